# Optimizing a Trainium2 kernel written in Bass

```python
import jax, jax.numpy as jnp
from jax import lax
import numpy as np

D_MODEL = 1024
BATCH = 8
SEQ = 2048
DEPTH = 1

N_RET_HEADS = 8
RET_HEAD_DIM = D_MODEL // N_RET_HEADS
RET_WIDTH = N_RET_HEADS * RET_HEAD_DIM
RET_CHUNK = 128
ROPE_BASE = 10000.0
CONV_WIDTH = D_MODEL
CONV_GROUPS = 8
CONV_K = 3
FFN_HIDDEN = -(-8 * D_MODEL // (3 * 256)) * 256
EPS = 1e-6
N_MOD = 6
IN_SPLITS = [RET_WIDTH, RET_WIDTH, RET_WIDTH, RET_WIDTH,
             CONV_WIDTH, CONV_WIDTH, CONV_WIDTH,
             D_MODEL, D_MODEL]
IN_COLS = sum(IN_SPLITS)

kernel_name = "hybrid_retention_shortconv_block"


def rmsnorm(x, g):
    xf = x.astype(jnp.float32)
    y = xf * lax.rsqrt(jnp.mean(xf * xf, axis=-1, keepdims=True) + EPS)
    return (y * g.astype(jnp.float32)).astype(x.dtype)


def head_layernorm(y):
    yf = y.astype(jnp.float32)
    mu = jnp.mean(yf, axis=-1, keepdims=True)
    var = jnp.mean((yf - mu) ** 2, axis=-1, keepdims=True)
    return ((yf - mu) * lax.rsqrt(var + EPS)).astype(y.dtype)


def modulate(h, shift, scale):
    return h * (1.0 + scale[:, None, :]) + shift[:, None, :]


def rope(t, cos, sin):
    t1, t2 = jnp.split(t, 2, axis=-1)
    out = jnp.concatenate([t1 * cos - t2 * sin, t1 * sin + t2 * cos], axis=-1)
    return out.astype(t.dtype)


def retention_chunkwise(q, k, v, log_gamma):
    B, S, H, d = q.shape
    N = S // RET_CHUNK

    def to_chunks(t):
        return t.reshape(B, N, RET_CHUNK, H, t.shape[-1]).transpose(0, 3, 1, 2, 4)

    qc, kc, vc = to_chunks(q), to_chunks(k), to_chunks(v)
    idx = jnp.arange(RET_CHUNK, dtype=jnp.float32)
    rel = idx[:, None] - idx[None, :]
    lg = log_gamma[:, None, None]
    dmask = jnp.where(rel >= 0, jnp.exp(lg * jnp.maximum(rel, 0.0)), 0.0)
    scores = jnp.einsum('bhncd,bhnmd->bhncm', qc, kc) * dmask[None, :, None]
    inner = jnp.einsum('bhncm,bhnme->bhnce', scores, vc)
    zeta = jnp.exp(log_gamma[:, None] * (RET_CHUNK - 1 - idx)[None, :])
    kv = jnp.einsum('bhncd,bhnce->bhnde', kc * zeta[None, :, None, :, None], vc)
    chunk_decay = jnp.exp(log_gamma * RET_CHUNK)[None, :, None, None]

    def step(R, kv_n):
        return R * chunk_decay + kv_n, R

    _, R_prev = lax.scan(step, jnp.zeros_like(kv[:, :, 0]), kv.transpose(2, 0, 1, 3, 4))
    R_prev = R_prev.transpose(1, 2, 0, 3, 4)
    xi = jnp.exp(log_gamma[:, None] * (idx + 1.0)[None, :])
    cross = jnp.einsum('bhncd,bhnde->bhnce', qc, R_prev) * xi[None, :, None, :, None]
    out = inner + cross
    return out.transpose(0, 2, 3, 1, 4).reshape(B, S, H, vc.shape[-1])


def setup_inputs(seed: int = 0) -> dict:
    key = jax.random.key(seed)
    ks = jax.random.split(key, 20)
    D = D_MODEL
    nrm = lambda k, shape, fan_in, s=1.0: s * jax.random.normal(k, shape, jnp.float32) * fan_in ** -0.5
    x = jax.random.normal(ks[0], (BATCH, SEQ, D), jnp.float32)
    c = jax.random.normal(ks[1], (BATCH, D), jnp.float32)
    positions = jnp.broadcast_to(jnp.arange(SEQ, dtype=jnp.int32)[None, :], (BATCH, SEQ))
    return {
        "x": x,
        "c": c,
        "positions": positions,
        "ada_w": nrm(ks[2], (DEPTH, D, N_MOD * D), D, 0.1),
        "ada_b": 0.02 * jax.random.normal(ks[3], (DEPTH, N_MOD * D), jnp.float32),
        "norm_mix_g": 1.0 + 0.02 * jax.random.normal(ks[4], (DEPTH, D), jnp.float32),
        "w_in": nrm(ks[5], (DEPTH, D, IN_COLS), D),
        "conv_w": nrm(ks[6], (DEPTH, CONV_K, CONV_WIDTH), CONV_K),
        "ret_w_out": nrm(ks[7], (DEPTH, RET_WIDTH, D), RET_WIDTH),
        "conv_w_out": nrm(ks[8], (DEPTH, CONV_WIDTH, D), CONV_WIDTH),
        "mix_w_out": nrm(ks[9], (DEPTH, D, D), D),
        "norm_ffn_g": 1.0 + 0.02 * jax.random.normal(ks[10], (DEPTH, D), jnp.float32),
        "ffn_w_gate": nrm(ks[11], (DEPTH, D, FFN_HIDDEN), D),
        "ffn_w_up": nrm(ks[12], (DEPTH, D, FFN_HIDDEN), D),
        "ffn_w_down": nrm(ks[13], (DEPTH, FFN_HIDDEN, D), FFN_HIDDEN),
        "final_norm_g": 1.0 + 0.02 * jax.random.normal(ks[14], (D,), jnp.float32),
    }


def reference(x, c, positions, ada_w, ada_b, norm_mix_g, w_in, conv_w, ret_w_out,
              conv_w_out, mix_w_out, norm_ffn_g, ffn_w_gate, ffn_w_up, ffn_w_down,
              final_norm_g):
    B, S, D = x.shape
    H, d = N_RET_HEADS, RET_HEAD_DIM
    log_gamma = jnp.log(1.0 - 2.0 ** (-5.0 - jnp.arange(H, dtype=jnp.float32)))
    inv_freq = 1.0 / (ROPE_BASE ** (jnp.arange(0, d, 2, dtype=jnp.float32) / d))
    ang = positions.astype(jnp.float32)[..., None] * inv_freq
    cos, sin = jnp.cos(ang)[:, :, None, :], jnp.sin(ang)[:, :, None, :]
    split_pts = list(np.cumsum(IN_SPLITS)[:-1])
    cs = jax.nn.silu(c)

    for l in range(DEPTH):
        mod = cs @ ada_w[l] + ada_b[l]
        sh_m, sc_m, gt_m, sh_f, sc_f, gt_f = jnp.split(mod, N_MOD, axis=-1)

        h = modulate(rmsnorm(x, norm_mix_g[l]), sh_m, sc_m)
        proj = h @ w_in[l]
        q, k, v, g_ret, b_cv, c_cv, u_cv, g_a, g_b = jnp.split(proj, split_pts, axis=-1)

        q = rope(q.reshape(B, S, H, d), cos, sin)
        k = rope(k.reshape(B, S, H, d), cos, sin) * (d ** -0.5)
        v = v.reshape(B, S, H, d)
        y_ret = head_layernorm(retention_chunkwise(q, k, v, log_gamma)).reshape(B, S, RET_WIDTH)
        y_a = (jax.nn.silu(g_ret) * y_ret) @ ret_w_out[l]

        u = c_cv * u_cv
        u_pad = jnp.pad(u, ((0, 0), (CONV_K - 1, 0), (0, 0)))
        w = conv_w[l]
        conv = u_pad[:, 0:S] * w[0]
        for tap in range(1, CONV_K):
            conv = conv + u_pad[:, tap:tap + S] * w[tap]
        y_b = (b_cv * conv) @ conv_w_out[l]

        merged = jax.nn.sigmoid(g_a) * y_a + jax.nn.sigmoid(g_b) * y_b
        x = x + gt_m[:, None, :] * (merged @ mix_w_out[l])

        h = modulate(rmsnorm(x, norm_ffn_g[l]), sh_f, sc_f)
        f = (jax.nn.silu(h @ ffn_w_gate[l]) * (h @ ffn_w_up[l])) @ ffn_w_down[l]
        x = x + gt_f[:, None, :] * f

    return rmsnorm(x, final_norm_g)
```

```python
import numpy as np
from contextlib import ExitStack
import concourse.bass as bass
import concourse.mybir as mybir
from concourse.bass_utils import run_bass_kernel_spmd

F32 = mybir.dt.float32
BF16 = mybir.dt.bfloat16
I32 = mybir.dt.int32
AF = mybir.ActivationFunctionType
ALU = mybir.AluOpType

P = 128
SEQ = 2048
NT = 16
D = 1024
KC = 8
H = 8
HID = 2816
NHC = 22
EPS = 1e-6
TWO_PI = 2.0 * np.pi

ENGS = ("pe", "act", "dve", "pool", "sp")
import os
CUT = int(os.environ.get("CUT", "1000000000"))
NOSELF = os.environ.get("NOSELF", "")


class Buf:
    __slots__ = ("name", "writers", "readers", "excl", "last")

    def __init__(self, name, excl=False):
        self.name = name
        self.writers = []
        self.readers = []
        self.excl = excl
        self.last = {}


class Op:
    __slots__ = ("eng", "fn", "deps", "idx", "ticket", "dma_key", "dma_val", "needs_inc", "is_dma")


class Sched:
    def __init__(self):
        self.ops = []
        self.q = {e: [] for e in ENGS}
        self.dma_counts = {}

    def add(self, eng, fn, reads=(), writes=(), dma_key=None, extra_deps=()):
        op = Op()
        op.eng = eng
        op.fn = fn
        op.idx = len(self.ops)
        op.is_dma = dma_key is not None
        op.dma_key = dma_key
        op.needs_inc = False
        op.ticket = None
        op.dma_val = None
        deps = []
        for b in reads:
            deps.extend(b.writers)
        for b in writes:
            if b.readers:
                deps.extend(b.readers)
                deps.extend(b.writers)
        deps.extend(extra_deps)
        for b in list(reads) + list(writes):
            if b.excl:
                for e2, o2 in b.last.items():
                    if e2 != eng:
                        deps.append(o2)
        best_e = {}
        best_d = {}
        for d in deps:
            if d is op:
                continue
            if d.is_dma:
                if d.dma_key not in best_d or best_d[d.dma_key].dma_val < d.dma_val:
                    best_d[d.dma_key] = d
            else:
                if d.eng not in best_e or best_e[d.eng].idx < d.idx:
                    best_e[d.eng] = d
        op.deps = list(best_e.values()) + list(best_d.values())
        if op.is_dma:
            self.dma_counts[dma_key] = self.dma_counts.get(dma_key, 0) + 16
            op.dma_val = self.dma_counts[dma_key]
        for b in list(reads) + list(writes):
            if b.excl:
                b.last[eng] = op
        for b in reads:
            b.readers.append(op)
        for b in writes:
            if b.readers:
                b.readers = []
                b.writers = [op]
            else:
                b.writers.append(op)
        self.ops.append(op)
        self.q[eng].append(op)
        return op

    def emit(self, nc, final_waits=()):
        for op in self.ops:
            for d in op.deps:
                if not d.is_dma:
                    d.needs_inc = True
        for e in ENGS:
            t = 0
            for op in self.q[e]:
                if not op.is_dma and op.needs_inc:
                    t += 1
                    op.ticket = t
        with ExitStack() as es:
            esem = {e: es.enter_context(nc.semaphore("s_" + e)) for e in ENGS}
            dsem = {k: es.enter_context(nc.semaphore("d_%s" % (k,))) for k in self.dma_counts}
            block = es.enter_context(nc.Block())

            def make(e):
                def body(engine):
                    seen_e = {x: 0 for x in ENGS}
                    seen_d = {k: 0 for k in self.dma_counts}
                    for op in self.q[e]:
                        if op.idx >= CUT:
                            break
                        for d in op.deps:
                            if d.is_dma:
                                if seen_d[d.dma_key] < d.dma_val:
                                    engine.wait_ge(dsem[d.dma_key], d.dma_val)
                                    seen_d[d.dma_key] = d.dma_val
                            else:
                                if d.eng == e and NOSELF and e in NOSELF.split(","):
                                    continue
                                if seen_e[d.eng] < d.ticket:
                                    engine.wait_ge(esem[d.eng], d.ticket)
                                    seen_e[d.eng] = d.ticket
                        ins = op.fn(engine)
                        if op.is_dma:
                            ins.then_inc(dsem[op.dma_key], 16)
                        elif op.needs_inc:
                            ins.then_inc(esem[e], 1)
                    if e == "sp":
                        fin = {}
                        for d in final_waits:
                            if d.idx >= CUT:
                                continue
                            fin[d.dma_key] = max(fin.get(d.dma_key, 0), d.dma_val)
                        if CUT < 1000000000:
                            for op2 in self.ops:
                                if op2.is_dma and op2.idx < CUT:
                                    fin[op2.dma_key] = max(fin.get(op2.dma_key, 0), op2.dma_val)
                        for k, v in fin.items():
                            if seen_d[k] < v:
                                engine.wait_ge(dsem[k], v)
                                seen_d[k] = v
                return body

            block.tensor(make("pe"))
            block.scalar(make("act"))
            block.vector(make("dve"))
            block.gpsimd(make("pool"))
            block.sync(make("sp"))


def handoff(old_bufs, new_bufs):
    users = []
    for b in old_bufs:
        users.extend(b.readers)
        users.extend(b.writers)
    for b in new_bufs:
        b.readers = list(b.readers) + users


FFN_GROUPS = [(0, 4), (4, 4), (8, 4), (12, 4), (16, 3), (19, 3)]


def build_nc(stage=99, dbg=False):
    nc = bass.Bass("TRN2", target_bir_lowering=False)

    def din(name, shape, dt=F32):
        return nc.dram_tensor(name, shape, dt, kind="ExternalInput").ap()

    x_d = din("x", [SEQ, D])
    ccol_d = din("ccol", [P, KC])
    pos_d = din("pos", [P, NT], I32)
    adaw_d = din("adaw_l", [12 * P, 4096])
    adabc_d = din("adab_col", [P, 48])
    adabr_d = din("adab_row", [1, 6 * D])
    gmix_d = din("gmix_col", [P, KC])
    gffn_d = din("gffn_col", [P, KC])
    gfin_d = din("gfin_row", [1, D])
    w1_d = din("w1_l", [8 * P, 4096])
    wc_d = din("wc_l", [8 * P, 3072])
    wp2_d = din("wp2_l", [8 * P, 4096])
    mixl_d = din("mix_l", [2 * P, 4096])
    convw_d = din("convw_col", [P, 8 * 3])
    wgu_d = [din("wgu_l%d" % g, [P, 2048 * n]) for g, (st_, n) in enumerate(FFN_GROUPS)]
    wd_d = din("ffn_w_down", [HID, D])
    invf_d = din("invf", [P, 64])
    mask_d = din("maskT", [P, H * P])
    zeta_d = din("zeta", [P, H])
    epsp_d = din("epsp", [P, H])
    ident_d = din("ident_in", [P, P])
    out_d = nc.dram_tensor("out", [SEQ, D], F32, kind="ExternalOutput").ap()
    if dbg:
        dbg_d = nc.dram_tensor("dbg", [P, 32768], BF16, kind="ExternalOutput").ap()

    S = Sched()
    global _LAST_SCHED
    _LAST_SCHED = S
    decay = [float(np.exp(np.float64(np.log(1.0 - 2.0 ** (-5.0 - h))) * 128.0)) for h in range(H)]

    with ExitStack() as es:
        def sb(name, shape, dt=F32):
            return es.enter_context(nc.sbuf_tensor(name, shape, dt))

        hT = sb("hT", [P, KC, SEQ], BF16)
        BC = sb("BC", [P, 16384], F32)
        BCb = BC[:, :].bitcast(BF16)
        ygT = BCb[:, 0:16384].rearrange("p (h s) -> p h s", h=KC)
        bcT = BCb[:, 16384:32768].rearrange("p (h s) -> p h s", h=KC)
        x1 = BC[:, :].rearrange("p (t d) -> p t d", t=NT)
        E = sb("E", [P, 8192], F32)
        Eb = E[:, :].bitcast(BF16)
        mergedT = Eb.rearrange("p (h s) -> p h s", h=KC)
        ring = sb("ring", [P, 16384], BF16)
        U = sb("U", [P, 9216], F32)
        Ub = U[:, :].bitcast(BF16)

        ident = sb("ident", [P, P])
        identb = sb("identb", [P, P], BF16)
        gtm = U[:, 0:1024]
        gtf = U[:, 1024:2048]
        gfin = U[:, 2048:3072]
        cols = sb("cols", [P, 256])
        csrep = sb("csrep", [P, KC, P], BF16)
        Rst = sb("Rst", [P, P])
        Rb = sb("Rb", [P, 4, P], BF16)
        posi = sb("posi", [P, NT], I32)
        tmp8 = sb("tmp8", [P, 8])
        sqj = sb("sqj", [P, D], BF16)
        Bsqj = Buf("sqj")
        sqj2 = sb("sqj2", [P, D], BF16)
        Bsqj2 = Buf("sqj2")
        dbgt = sb("dbgt", [P, 128])

        c_ccol = cols[:, 0:8]
        c_cs = cols[:, 8:16]
        c_gmix = cols[:, 16:24]
        c_gffn = cols[:, 24:32]
        c_adab = cols[:, 32:80]
        c_convw = cols[:, 80:104]
        c_zeta = cols[:, 104:112]
        c_epsp = cols[:, 112:120]
        c_A1 = cols[:, 120:128]
        c_B1 = cols[:, 128:136]
        c_A2 = cols[:, 136:144]
        c_B2 = cols[:, 144:152]
        c_mhalf = cols[:, 152:153]
        c_posf = cols[:, 160:176]
        c_ss = cols[:, 176:192]
        c_rs = cols[:, 192:208]
        c_tmp = cols[:, 208:224]
        c_ln = cols[:, 224:256]
        csbf = sb("csbf", [P, KC], BF16)
        lnst = sb("lnst", [P, 8, 8])

        cosT = U[:, 0:1024].rearrange("p (t j) -> p t j", t=NT)
        sinT = U[:, 1024:2048].rearrange("p (t j) -> p t j", t=NT)
        nsinT = U[:, 2048:3072].rearrange("p (t j) -> p t j", t=NT)
        maskT = U[:, 3072:4096].rearrange("p (h c) -> p h c", h=H)
        p1 = BC[:, 0:4096]
        cvt = BC[:, 4096:8192]
        PS = [es.enter_context(nc.psum_tensor("ps%d" % i, [P, 512], F32)) for i in range(8)]
        BPS = [Buf("ps%d" % i, excl=True) for i in range(8)]

        Bconst = Buf("const")
        BhT = [Buf("hT%d" % t) for t in range(NT)]
        Byg = [Buf("yg%d" % i) for i in range(4)]
        Bbc = [Buf("bc%d" % i) for i in range(4)]
        Bx1 = [Buf("x1_%d" % t) for t in range(NT)]
        Bmg = [Buf("mg%d" % i) for i in range(4)]
        Bring = [Buf("ring%d" % i) for i in range(4)]
        Bcols = Buf("cols_setup")
        BAB1 = Buf("AB1")
        BAB2 = Buf("AB2")
        Bgtm = Buf("gtm")
        Bgtf = Buf("gtf")
        Bgfin = Buf("gfin")
        Brope = Buf("rope")
        Bcsrep = Buf("csrep")

        def ring_slot(s):
            return ring[:, s * 4096:(s + 1) * 4096].rearrange("p (k n) -> p k n", k=KC)

        def ring_big(b):
            return ring[:, b * 8192:(b + 1) * 8192].rearrange("p (k n) -> p k n", k=KC)

        def wview(w_ap):
            return w_ap.rearrange("(k p) n -> p k n", p=P)

        ADD = S.add

        cdma_n = [0]

        def cdma(out_ap, in_ap):
            q = "sp" if cdma_n[0] % 2 == 0 else "act"
            cdma_n[0] += 1
            ADD(q, lambda e: e.dma_start(out=out_ap, in_=in_ap), writes=[Bconst], dma_key="const")

        cdma(c_ccol, ccol_d[:, :])
        cdma(posi[:, :], pos_d[:, :])
        cdma(c_adab, adabc_d[:, :])
        cdma(c_gmix, gmix_d[:, :])
        cdma(c_gffn, gffn_d[:, :])
        cdma(c_convw, convw_d[:, :])
        cdma(c_zeta, zeta_d[:, :])
        cdma(c_epsp, epsp_d[:, :])
        cdma(ident[:, :], ident_d[:, :])
        cdma(p1[:, 0:64], invf_d[:, :])
        cdma(maskT, mask_d[:, :].rearrange("p (h c) -> p h c", h=H))

        wload_extra = [()]

        def wload(dst_flat, src_flat, bufs, key):
            L = dst_flat.shape[1]
            nch = (L + 2047) // 2048
            assert L % nch == 0
            d3 = dst_flat.rearrange("p (c n) -> p c n", c=nch)
            s3 = src_flat.rearrange("p (c n) -> p c n", c=nch)
            return ADD("pool", lambda e: e.dma_start(out=d3, in_=s3), writes=bufs, dma_key=key, extra_deps=wload_extra[0])

        def ring_flat(s, L=4096):
            return ring[:, s * 4096:s * 4096 + L]

        def load_ada(blk, slot):
            wload(ring_flat(slot), adaw_d[blk * P:(blk + 1) * P, :], [Bring[slot]], "ring%d" % slot)

        ada_first = []
        for blk in range(4):
            load_ada(blk, blk)
            ada_first.append(S.ops[-1])

        ADD("dve", lambda e: e.tensor_copy(out=identb[:, :], in_=ident[:, :]), reads=[Bconst], writes=[Bcols])
        ADD("dve", lambda e: e.memset(c_mhalf, -0.5), writes=[Bcols])
        ADD("act", lambda e: e.activation(out=c_cs, in_=c_ccol, func=AF.Silu), reads=[Bconst], writes=[Bcols])
        Bcsbf = Buf("csbf")
        ADD("dve", lambda e: e.tensor_copy(out=csbf[:, :], in_=c_cs), reads=[Bcols], writes=[Bcsbf])
        ADD("dve", lambda e: e.tensor_copy(out=csrep[:, :, :], in_=c_cs.unsqueeze(2).broadcast_to([P, KC, P])),
            reads=[Bcols], writes=[Bcsrep])
        ADD("dve", lambda e: e.tensor_copy(out=c_posf, in_=posi[:, :]), reads=[Bconst], writes=[Bcols])

        ang = p1[:, 64:1088].rearrange("p (t j) -> p t j", t=NT)
        uu = p1[:, 1088:2112].rearrange("p (t j) -> p t j", t=NT)
        ki = cvt[:, 0:1024].bitcast(I32).rearrange("p (t j) -> p t j", t=NT)
        ff = cvt[:, 1024:2048].rearrange("p (t j) -> p t j", t=NT)
        invf = p1[:, 0:64]
        ADD("dve", lambda e: e.memset(Rb[:, :, :], 0.0), writes=[Bcols])

        def ada_cols(blk, ps_cols, slot):
            for j in range(4):
                for kc in range(KC):
                    ADD("pe", lambda e, j=j, kc=kc: e.matmul(PS[4][:, ps_cols + j:ps_cols + j + 1],
                                                             lhsT=ring_slot(slot)[:, kc, j * P:(j + 1) * P],
                                                             rhs=csbf[:, kc:kc + 1], start=(kc == 0), stop=(kc == KC - 1)),
                        reads=[Bring[slot], Bcsbf], writes=[BPS[4]])

        def ada_bcast(blk_half, dst, dstbuf, slot, bank):
            for kc in range(KC):
                ADD("pe", lambda e, kc=kc: e.matmul(PS[bank][:, :], lhsT=csrep[:, kc, :], rhs=ring_slot(slot)[:, kc, :],
                                                    start=(kc == 0), stop=(kc == KC - 1)),
                    reads=[Bring[slot], Bcsrep], writes=[BPS[bank]])
            sl = slice(blk_half * 512, (blk_half + 1) * 512)
            ADD("dve", lambda e: e.tensor_tensor(out=dst[:, sl], in0=PS[bank][:, :], in1=dst[:, sl], op=ALU.add),
                reads=[BPS[bank], dstbuf], writes=[dstbuf])

        for blk in range(4):
            ada_cols(blk, blk * 4, blk)
        ADD("dve", lambda e: e.tensor_tensor(out=c_B1, in0=PS[4][:, 0:8], in1=c_adab[:, 0:8], op=ALU.add),
            reads=[BPS[4], Bconst], writes=[BAB1])
        ADD("dve", lambda e: e.tensor_tensor(out=tmp8[:, :], in0=PS[4][:, 8:16], in1=c_adab[:, 8:16], op=ALU.add),
            reads=[BPS[4], Bconst], writes=[BAB1])
        ADD("dve", lambda e: e.scalar_tensor_tensor(out=c_A1, in0=tmp8[:, :], scalar=1.0, in1=c_gmix,
                                                    op0=ALU.add, op1=ALU.mult), reads=[BAB1, Bconst], writes=[BAB1])

        ADD("dve", lambda e: e.tensor_tensor(out=ang, in0=c_posf.unsqueeze(2).broadcast_to([P, NT, 64]),
                                              in1=invf.unsqueeze(1).broadcast_to([P, NT, 64]), op=ALU.mult),
            reads=[Bcols, Bconst], writes=[Brope])
        for (dst, shift) in ((sinT, 0.5), (cosT, 0.75)):
            ADD("dve", lambda e, shift=shift: e.tensor_scalar(out=uu, in0=ang, scalar1=float(1.0 / TWO_PI), scalar2=shift,
                                                               op0=ALU.mult, op1=ALU.add), reads=[Brope], writes=[Brope])
            ADD("dve", lambda e: e.tensor_copy(out=ki, in_=uu), reads=[Brope], writes=[Brope])
            ADD("dve", lambda e: e.tensor_tensor(out=ff, in0=uu, in1=ki, op=ALU.subtract), reads=[Brope], writes=[Brope])
            ADD("dve", lambda e: e.tensor_scalar(out=uu, in0=ff, scalar1=0.0, scalar2=None, op0=ALU.is_lt),
                reads=[Brope], writes=[Brope])
            ADD("dve", lambda e: e.tensor_tensor(out=ff, in0=ff, in1=uu, op=ALU.add), reads=[Brope], writes=[Brope])
            ADD("dve", lambda e: e.tensor_scalar(out=ff, in0=ff, scalar1=1.0, scalar2=0.0, op0=ALU.min, op1=ALU.max),
                reads=[Brope], writes=[Brope])
            ADD("act", lambda e, dst=dst: e.activation(out=dst, in_=ff, func=AF.Sin, bias=float(-np.pi), scale=float(TWO_PI)),
                reads=[Brope, Bcols], writes=[Brope])
        ADD("dve", lambda e: e.tensor_scalar(out=nsinT, in0=sinT, scalar1=-1.0, scalar2=None, op0=ALU.mult),
            reads=[Brope], writes=[Brope])

        tp_par = [0]

        def norm_A(t, src_ap, src_bufs, xn_ap, xn_buf, inplace=False):
            if inplace and t % 2 == 1:
                ADD("act", lambda e: e.activation(out=sqj2[:, :], in_=src_ap, func=AF.Square, accum_out=c_ss[:, t:t + 1]),
                    reads=src_bufs + [Bsqj2], writes=[Bsqj2, Bss[t]])
            elif inplace:
                ADD("dve", lambda e: e.scalar_tensor_tensor(out=sqj[:, :], in0=src_ap, scalar=1.0, in1=src_ap, op0=ALU.mult,
                                                            op1=ALU.mult, accum_out=c_ss[:, t:t + 1]),
                    reads=src_bufs + [Bsqj], writes=[Bsqj, Bss[t]])
            else:
                ADD("dve", lambda e: e.scalar_tensor_tensor(out=xn_ap, in0=src_ap, scalar=1.0, in1=src_ap, op0=ALU.mult,
                                                            op1=ALU.mult, accum_out=c_ss[:, t:t + 1]),
                    reads=src_bufs, writes=[xn_buf, Bss[t]])
            ADD("dve", lambda e: e.tensor_scalar(out=c_tmp[:, t:t + 1], in0=c_ss[:, t:t + 1], scalar1=float(1.0 / D),
                                                 scalar2=float(EPS), op0=ALU.mult, op1=ALU.add),
                reads=[Bss[t]], writes=[Bss[t]])
            ADD("pool", lambda e: e.tensor_tensor(out=c_rs[:, t:t + 1], in0=c_tmp[:, t:t + 1], in1=c_mhalf, op=ALU.pow),
                reads=[Bss[t], Bcols], writes=[Bss[t]])
            ADD("act", lambda e: e.activation(out=xn_ap, in_=src_ap, func=AF.Identity, scale=c_rs[:, t:t + 1]),
                reads=src_bufs + [Bss[t]], writes=[xn_buf])

        def norm_B_block(tb, xn_aps, xn_bufs, Acol, Bcol, ABbuf):
            for half in range(2):
                for kq in range(4):
                    kc = half * 4 + kq
                    bank = kc
                    for i in range(4):
                        ADD("pe", lambda e, kc=kc, bank=bank, i=i: e.transpose(out=PS[bank][:, i * P:(i + 1) * P],
                                                                               in_=xn_aps[i][:, kc * P:(kc + 1) * P], identity=ident[:, :]),
                            reads=[xn_bufs[i], Bconst], writes=[BPS[bank]])
                for kq in range(4):
                    kc = half * 4 + kq
                    bank = kc
                    dst = hT[:, kc, tb * 512:(tb + 1) * 512]
                    if half == 0:
                        ADD("act", lambda e, bank=bank, dst=dst, kc=kc: e.activation(out=dst, in_=PS[bank][:, :], func=AF.Identity,
                                                                                     bias=Bcol[:, kc:kc + 1], scale=Acol[:, kc:kc + 1]),
                            reads=[BPS[bank], ABbuf], writes=BhT[tb * 4:(tb + 1) * 4])
                    else:
                        ADD("dve", lambda e, bank=bank, dst=dst, kc=kc: e.tensor_scalar(out=dst, in0=PS[bank][:, :], scalar1=Acol[:, kc:kc + 1],
                                                                                        scalar2=Bcol[:, kc:kc + 1], op0=ALU.mult, op1=ALU.add),
                            reads=[BPS[bank], ABbuf], writes=BhT[tb * 4:(tb + 1) * 4])

        def norm_schedule(A_fn, B_fn, R=6):
            done_B = -1
            nextA = 0
            for tb in range(4):
                while nextA < NT and (nextA - R) // 4 <= done_B and nextA < 4 * tb + R:
                    A_fn(nextA)
                    nextA += 1
                B_fn(tb)
                done_B = tb
            assert nextA == NT

        Bss = [Buf("ss%d" % t) for t in range(NT)]

        Bxs = []
        Bxn = [Buf("xn%d" % i) for i in range(8)]
        xn_slot = [E[:, i * 1024:(i + 1) * 1024] for i in range(8)]

        def p0_A(t):
            s_ = t % 8
            ADD("sp", lambda e, s_=s_, t=t: e.dma_start(out=xn_slot[s_], in_=x_d[t * P:(t + 1) * P, :]),
                writes=[Bxn[s_]], dma_key="xn%d" % s_, extra_deps=(ada_first if t >= 2 else ()))
            norm_A(t, xn_slot[s_], [Bxn[s_]], xn_slot[s_], Bxn[s_], inplace=True)

        def p0_B(tb):
            ts_ = range(tb * 4, tb * 4 + 4)
            norm_B_block(tb, [xn_slot[t % 8] for t in ts_], [Bxn[t % 8] for t in ts_], c_A1, c_B1, BAB1)

        norm_schedule(p0_A, p0_B, R=8)

        final_waits = []
        dbg_ops = []

        def finish():
            S.emit(nc, final_waits=final_waits + dbg_ops)

        if stage == 0:
            if dbg:
                dbg_ops.append(ADD("sp", lambda e: e.dma_start(out=dbg_d[:, 0:16384], in_=hT[:, :, :].rearrange("p k s -> p (k s)")),
                                   reads=BhT, dma_key="dbg"))
            finish()
            return nc

        handoff([Brope] + Bxs + Bxn, [])
        def load_qkvg(h):
            s_ = h % 2
            wload(ring_flat(s_), w1_d[h * P:(h + 1) * P, :], [Bring[s_]], "ring%d" % s_)

        def load_conv(g):
            cs_ = 2 + g % 2
            wload(ring_flat(cs_, 3072), wc_d[g * P:(g + 1) * P, :], [Bring[cs_]], "ring%d" % cs_)

        load_qkvg(0)
        load_conv(0)
        load_qkvg(1)

        def mk(off, width, depth, dt=F32):
            aps = []
            for i in range(depth):
                a = E[:, off + i * width: off + (i + 1) * width]
                if dt == BF16:
                    a = a.bitcast(BF16)
                aps.append(a)
            return aps, off + depth * width
        o = 0
        tA, o = mk(o, 256, 2)
        tB, o = mk(o, 256, 2)
        tQK, o = mk(o, 128, 4, BF16)
        tvb, o = mk(o, 64, 4, BF16)
        tvz, o = mk(o, 64, 4, BF16)
        tsg, o = mk(o, 128, 8)
        tqkT, o = mk(o, 128, 4, BF16)
        tsm, o = mk(o, 64, 4, BF16)
        ty, o = mk(o, 128, 4)
        tyg, o = mk(o, 64, 4, BF16)
        assert o <= 5120
        cv_c = E[:, 5120:5632]
        cv_u = [E[:, 5632:6148], E[:, 6148:6664]]
        cv_t = E[:, 6664:7176]
        cv_b = E[:, 7176:7688]
        BtA = [Buf("tA%d" % i) for i in range(2)]
        BtB = [Buf("tB%d" % i) for i in range(2)]
        BtQK = [Buf("tQK%d" % i) for i in range(4)]
        Btvb = [Buf("tvb%d" % i) for i in range(4)]
        Btvz = [Buf("tvz%d" % i) for i in range(4)]
        Btsg = [Buf("tsg%d" % i) for i in range(8)]
        BtqkT = [Buf("tqkT%d" % i) for i in range(4)]
        Btsm = [Buf("tsm%d" % i) for i in range(4)]
        Bty = [Buf("ty%d" % i) for i in range(4)]
        Btyg = [Buf("tyg%d" % i) for i in range(4)]
        Bln = [Buf("ln%d" % i) for i in range(8)]
        Bcvc, Bcvt, Bcvb = Buf("cvc"), Buf("cvt"), Buf("cvb")
        Bcvu = [Buf("cvu0"), Buf("cvu1")]
        BRst = Buf("Rst")
        BRb = [Buf("Rb%d" % i) for i in range(4)]
        e_p1 = BtA + BtB + BtQK + Btvb + Btvz + Btsg + BtqkT + Btsm + Bty + Btyg + [Bcvc, Bcvt, Bcvb] + Bcvu
        handoff(Bxs + Bxn, e_p1)
        handoff([Brope], Byg + Bbc)
        qkT_ps = [PS[2][:, 0:128].bitcast(BF16), PS[2][:, 128:256].bitcast(BF16)]
        ygps = [PS[2][:, 256 + i * 64:256 + (i + 1) * 64].bitcast(BF16) for i in range(4)]
        sc_ps = [PS[3][:, 0:128], PS[3][:, 256:384]]
        kv_ps = [PS[3][:, 128:256], PS[3][:, 384:512]]
        out_ps = [PS[4][:, 0:128], PS[4][:, 128:256]]
        Bqkps = [BPS[2], BPS[2]]
        Bygps = [BPS[2]] * 4
        Bscps = [BPS[3], BPS[3]]
        Bkvps = [BPS[3], BPS[3]]
        Boutps = [BPS[4], BPS[4]]
        Badaps = BPS[4]

        def S1(G, mid=None):
            h, t = divmod(G, NT)
            s = h % 2
            W = ring_slot(s)
            for half in range(2):
                for kc in range(KC):
                    ADD("pe", lambda e, kc=kc, half=half: e.matmul(PS[half][:, 0:256], lhsT=hT[:, kc, t * P:(t + 1) * P],
                                                                   rhs=W[:, kc, half * 256:(half + 1) * 256],
                                                                   start=(kc == 0), stop=(kc == KC - 1)),
                        reads=[BhT[t], Bring[s]], writes=[BPS[half]])
                if half == 0 and mid is not None:
                    mid()

        def S2(G):
            h, t = divmod(G, NT)
            pqa = PS[0]
            pqb = PS[1]
            T4 = pqa[:, 0:256].rearrange("p (a b j) -> p a b j", a=2, b=2)
            A4 = tA[G % 2].rearrange("p (a b j) -> p a b j", a=2, b=2)
            B4 = tB[G % 2].rearrange("p (a b j) -> p a b j", a=2, b=2)
            CC = cosT[:, t, :].unsqueeze(1).unsqueeze(1).broadcast_to([P, 2, 2, 64])
            SN = sinT[:, t, :].unsqueeze(1).broadcast_to([P, 2, 64])
            NS = nsinT[:, t, :].unsqueeze(1).broadcast_to([P, 2, 64])
            ADD("dve", lambda e: e.tensor_tensor(out=A4, in0=T4, in1=CC, op=ALU.mult),
                reads=[BPS[0], Brope], writes=[BtA[G % 2]])
            ADD("dve", lambda e: e.tensor_tensor(out=B4[:, :, 0, :], in0=T4[:, :, 1, :], in1=NS, op=ALU.mult),
                reads=[BPS[0], Brope], writes=[BtB[G % 2]])
            ADD("dve", lambda e: e.tensor_tensor(out=B4[:, :, 1, :], in0=T4[:, :, 0, :], in1=SN, op=ALU.mult),
                reads=[BPS[0], Brope], writes=[BtB[G % 2]])
            ADD("pool", lambda e: e.tensor_tensor(out=tQK[G % 4], in0=tA[G % 2], in1=tB[G % 2], op=ALU.add),
                reads=[BtA[G % 2], BtB[G % 2]], writes=[BtQK[G % 4]])
            ADD("act", lambda e: e.activation(out=tvb[G % 4], in_=pqb[:, 0:128], func=AF.Identity),
                reads=[BPS[1]], writes=[Btvb[G % 4]])
            ADD("act", lambda e: e.activation(out=tvz[G % 4], in_=pqb[:, 0:128], func=AF.Identity, scale=c_zeta[:, h:h + 1]),
                reads=[BPS[1], Bconst], writes=[Btvz[G % 4]])
            ADD("act", lambda e: e.activation(out=tsg[G % 8], in_=pqb[:, 128:256], func=AF.Silu),
                reads=[BPS[1]], writes=[Btsg[G % 8]])

        def S3(G):
            for j in range(2):
                ADD("pe", lambda e, j=j: e.transpose(out=qkT_ps[G % 2][:, j * P:(j + 1) * P], in_=tQK[G % 4][:, j * P:(j + 1) * P],
                                                     identity=identb[:, :]),
                    reads=[BtQK[G % 4], Bcols], writes=[Bqkps[G % 2]])
            ADD("act", lambda e: e.activation(out=tqkT[G % 4], in_=qkT_ps[G % 2], func=AF.Identity),
                reads=[Bqkps[G % 2]], writes=[BtqkT[G % 4]])

        def S5(G):
            h, t = divmod(G, NT)
            qk = tqkT[G % 4]
            ADD("pe", lambda e: e.matmul(sc_ps[G % 2], lhsT=qk[:, 128:256], rhs=qk[:, 0:128], start=True, stop=True),
                reads=[BtqkT[G % 4]], writes=[Bscps[G % 2]])
            ADD("pe", lambda e: e.matmul(kv_ps[G % 2], lhsT=tQK[G % 4][:, 128:256], rhs=tvz[G % 4], start=True, stop=True),
                reads=[BtQK[G % 4], Btvz[G % 4]], writes=[Bkvps[G % 2]])
            ADD("dve", lambda e: e.tensor_tensor(out=tsm[G % 4], in0=sc_ps[G % 2], in1=maskT[:, h, :], op=ALU.mult),
                reads=[Bscps[G % 2], Bconst], writes=[Btsm[G % 4]])
            if t == 0:
                ADD("dve", lambda e: e.tensor_copy(out=Rst[:, :], in_=kv_ps[G % 2]), reads=[Bkvps[G % 2]], writes=[BRst])
            elif t < NT - 1:
                ADD("dve", lambda e: e.scalar_tensor_tensor(out=Rst[:, :], in0=Rst[:, :], scalar=decay[h], in1=kv_ps[G % 2],
                                                            op0=ALU.mult, op1=ALU.add),
                    reads=[Bkvps[G % 2], BRst], writes=[BRst])
            if t < NT - 1:
                ADD("pool", lambda e: e.tensor_copy(out=Rb[:, (G + 1) % 4, :], in_=Rst[:, :]), reads=[BRst], writes=[BRb[(G + 1) % 4]])

        def S7(G):
            h, t = divmod(G, NT)
            ADD("pe", lambda e: e.matmul(out_ps[G % 2], lhsT=tsm[G % 4], rhs=tvb[G % 4], start=True, stop=(t == 0)),
                reads=[Btsm[G % 4], Btvb[G % 4]], writes=[Boutps[G % 2]])
            if t > 0:
                ADD("pe", lambda e: e.matmul(out_ps[G % 2], lhsT=tqkT[G % 4][:, 0:128], rhs=Rb[:, G % 4, :], start=False, stop=True),
                    reads=[BtqkT[G % 4], BRb[G % 4]], writes=[Boutps[G % 2]])
            sl = G % 8
            st = lnst[:, sl, 0:6]
            mv = lnst[:, sl, 6:8]
            lc = c_ln[:, sl * 4:(sl + 1) * 4]
            ADD("dve", lambda e: e.bn_stats(out=st, in_=out_ps[G % 2]), reads=[Boutps[G % 2]], writes=[Bln[sl]])
            ADD("dve", lambda e: e.bn_aggr(out=mv, in_=st), reads=[Bln[sl]], writes=[Bln[sl]])
            ADD("pool", lambda e: e.tensor_tensor(out=lc[:, 0:1], in0=mv[:, 1:2], in1=c_epsp[:, h:h + 1], op=ALU.add),
                reads=[Bln[sl], Bconst], writes=[Bln[sl]])
            ADD("pool", lambda e: e.tensor_tensor(out=lc[:, 1:2], in0=lc[:, 0:1], in1=c_mhalf, op=ALU.pow),
                reads=[Bln[sl], Bcols], writes=[Bln[sl]])

        def S8a(G):
            sl = G % 8
            mv = lnst[:, sl, 6:8]
            lc = c_ln[:, sl * 4:(sl + 1) * 4]
            ADD("dve", lambda e: e.tensor_scalar(out=ty[G % 4], in0=out_ps[G % 2], scalar1=mv[:, 0:1], scalar2=lc[:, 1:2],
                                                 op0=ALU.subtract, op1=ALU.mult),
                reads=[Boutps[G % 2], Bln[sl]], writes=[Bty[G % 4]])

        def S8b(G):
            ADD("pool", lambda e: e.tensor_tensor(out=tyg[G % 4], in0=ty[G % 4], in1=tsg[G % 8], op=ALU.mult),
                reads=[Bty[G % 4], Btsg[G % 8]], writes=[Btyg[G % 4]])

        def S9(G):
            h, t = divmod(G, NT)
            ADD("pe", lambda e: e.transpose(out=ygps[G % 4], in_=tyg[G % 4], identity=identb[:, :]),
                reads=[Btyg[G % 4], Bcols], writes=[Bygps[G % 4]])
            ADD("act", lambda e: e.activation(out=ygT[:, h, t * P:(t + 1) * P], in_=ygps[G % 4], func=AF.Identity),
                reads=[Bygps[G % 4]], writes=[Byg[t // 4]])

        def conv_slot(g):
            return 2 + g % 2

        def conv_pe(g, tb, part):
            cs_ = conv_slot(g)
            W = ring_flat(cs_, 3072).rearrange("p (k n) -> p k n", k=KC)
            seq = [(j, bank, kc) for j, bank in ((1, 6), (2, 7), (0, 5)) for kc in range(KC)]
            for j, bank, kc in seq[part * 6:(part + 1) * 6]:
                ADD("pe", lambda e, j=j, bank=bank, kc=kc: e.matmul(PS[bank][:, :], lhsT=W[:, kc, j * P:(j + 1) * P],
                                                                    rhs=hT[:, kc, tb * 512:(tb + 1) * 512],
                                                                    start=(kc == 0), stop=(kc == KC - 1)),
                    reads=[Bring[cs_]] + BhT[tb * 4:(tb + 1) * 4], writes=[BPS[bank]])

        def conv_ew_chunks(g, tb):
            cur, prv = tb % 2, (tb + 1) % 2
            w0 = c_convw[:, g * 3 + 0:g * 3 + 1]
            w1 = c_convw[:, g * 3 + 1:g * 3 + 2]
            w2 = c_convw[:, g * 3 + 2:g * 3 + 3]

            def chA():
                ADD("act", lambda e: e.activation(out=cv_c, in_=PS[6][:, :], func=AF.Identity), reads=[BPS[6]], writes=[Bcvc])
                if tb == 0:
                    ADD("pool", lambda e: e.memset(cv_u[cur][:, 0:2], 0.0), writes=[Bcvu[cur]])
                else:
                    ADD("pool", lambda e: e.tensor_copy(out=cv_u[cur][:, 0:2], in_=cv_u[prv][:, 512:514]),
                        reads=[Bcvu[prv]], writes=[Bcvu[cur]])
                ADD("dve", lambda e: e.tensor_tensor(out=cv_u[cur][:, 2:514], in0=PS[7][:, :], in1=cv_c, op=ALU.mult),
                    reads=[BPS[7], Bcvc], writes=[Bcvu[cur]])

            def chB():
                ADD("act", lambda e: e.activation(out=cv_b, in_=PS[5][:, :], func=AF.Identity), reads=[BPS[5]], writes=[Bcvb])
                ADD("act", lambda e: e.activation(out=cv_t, in_=cv_u[cur][:, 2:514], func=AF.Identity, scale=w2),
                    reads=[Bcvu[cur], Bconst], writes=[Bcvt])

            def chC():
                ADD("dve", lambda e: e.scalar_tensor_tensor(out=cv_t, in0=cv_u[cur][:, 1:513], scalar=w1, in1=cv_t,
                                                            op0=ALU.mult, op1=ALU.add),
                    reads=[Bcvu[cur], Bcvt, Bconst], writes=[Bcvt])
                ADD("dve", lambda e: e.scalar_tensor_tensor(out=cv_t, in0=cv_u[cur][:, 0:512], scalar=w0, in1=cv_t,
                                                            op0=ALU.mult, op1=ALU.add),
                    reads=[Bcvu[cur], Bcvt, Bconst], writes=[Bcvt])

            def chD():
                ADD("pool", lambda e: e.tensor_tensor(out=bcT[:, g, tb * 512:(tb + 1) * 512], in0=cv_b, in1=cv_t, op=ALU.mult),
                    reads=[Bcvb, Bcvt], writes=[Bbc[tb]])
            return [chA, chB, chC, chD]

        P2SLOT = [2, 3, 0, 1, 2, 3, 0, 1]

        def load_p2(dc):
            s_ = P2SLOT[dc]
            wload(ring_flat(s_), wp2_d[dc * P:(dc + 1) * P, :], [Bring[s_]], "ring%d" % s_)

        adax = Ub[:, 8192:12288]
        Badax = Buf("adax")

        def ada_f_block(i_):
            blk = 6 + i_
            wload(adax, adaw_d[blk * P:(blk + 1) * P, :], [Badax], "adax")

        def ada_f_mms(i_):
            W_ = adax.rearrange("p (k n) -> p k n", k=KC)
            for jj in range(4):
                for kc in range(KC):
                    ADD("pe", lambda e, jj=jj, kc=kc: e.matmul(PS[4][:, 256 + i_ * 4 + jj:256 + i_ * 4 + jj + 1],
                                                               lhsT=W_[:, kc, jj * P:(jj + 1) * P],
                                                               rhs=csbf[:, kc:kc + 1], start=(kc == 0), stop=(kc == KC - 1)),
                        reads=[Badax, Bcsbf], writes=[BPS[4]])

        NG1 = H * NT
        pending = {}
        for j in range(NG1 + 8):
            if j < NG1:
                h, t = divmod(j, NT)
                S1(j)
                S2(j)
            if 0 <= j - 7 < NG1:
                S9(j - 7)
            if j < NG1:
                conv_pe(h, t // 4, t % 4)
                if t % 4 == 3:
                    for ci_, ch in enumerate(conv_ew_chunks(h, t // 4)):
                        pending.setdefault(j + ci_, []).append(ch)
                if t == 4 and h + 1 < H:
                    if h >= 1:
                        load_qkvg(h + 1)
                    load_conv(h + 1)
                if h < 4 and t == 2:
                    ada_f_block(h)
                if h < 4 and t == 10:
                    ada_f_mms(h)
                if h == 7 and t == 8:
                    load_p2(0)
                if h == 7 and t == 12:
                    load_p2(2)
            if 0 <= j - 1 < NG1:
                S3(j - 1)
            if 0 <= j - 2 < NG1:
                S5(j - 2)
            if 0 <= j - 3 < NG1:
                S7(j - 3)
            if 0 <= j - 4 < NG1:
                S8a(j - 4)
            if 0 <= j - 5 < NG1:
                S8b(j - 5)
            for ch in pending.pop(j, []):
                ch()
        assert not pending
        ADD("dve", lambda e: e.tensor_tensor(out=c_B2, in0=PS[4][:, 256:264], in1=c_adab[:, 24:32], op=ALU.add),
            reads=[BPS[4], Bconst], writes=[BAB2])
        ADD("dve", lambda e: e.tensor_tensor(out=tmp8[:, :], in0=PS[4][:, 264:272], in1=c_adab[:, 32:40], op=ALU.add),
            reads=[BPS[4], Bconst, BAB1], writes=[BAB2])
        ADD("dve", lambda e: e.scalar_tensor_tensor(out=c_A2, in0=tmp8[:, :], scalar=1.0, in1=c_gffn,
                                                    op0=ALU.add, op1=ALU.mult), reads=[BAB2, Bconst], writes=[BAB2])
        if stage == 1:
            if dbg:
                dbg_ops.append(ADD("sp", lambda e: e.dma_start(out=dbg_d[:, 0:32768], in_=BCb[:, :]), reads=Byg + Bbc, dma_key="dbg"))
            finish()
            return nc

        handoff(e_p1, Bmg)
        p2t = [[U[:, 5120 + (2 * s + i) * 512: 5120 + (2 * s + i + 1) * 512] for i in range(2)] for s in range(2)]
        Bp2t = [[Buf("p2t%d%d" % (s, i)) for i in range(2)] for s in range(2)]
        handoff([Brope], [Bgtm, Bgtf, Bgfin])
        ADD("sp", lambda e: e.dma_start(out=gtm, in_=adabr_d[0:1, 2 * D:3 * D].broadcast_to([P, D])), writes=[Bgtm], dma_key="gtm")
        ADD("sp", lambda e: e.dma_start(out=gtf, in_=adabr_d[0:1, 5 * D:6 * D].broadcast_to([P, D])), writes=[Bgtf], dma_key="gtf")
        ADD("sp", lambda e: e.dma_start(out=gfin, in_=gfin_d[0:1, :].broadcast_to([P, D])), writes=[Bgfin], dma_key="gfin")
        cnt = 0
        load_p2(1)
        for dc in range(KC):
            if dc >= 1 and dc + 2 < KC:
                load_p2(dc + 2)
            if dc == 5:
                load_ada(4, 2)
            if dc == 6:
                load_ada(5, 3)
            if dc == 7:
                wload(ring_flat(0), mixl_d[0:P, :], [Bring[0]], "ring0")
            s = P2SLOT[dc]
            W = ring_slot(s)
            for tb in range(4):
                st = cnt % 2
                cnt += 1
                bk = [0, 1, 2, 3] if st == 0 else [4, 5, 6, 7]
                tsl = slice(tb * 512, (tb + 1) * 512)
                for kc in range(KC):
                    ADD("pe", lambda e, kc=kc, b=bk[0], tsl=tsl, W=W: e.matmul(PS[b][:, :], lhsT=W[:, kc, 0:128], rhs=ygT[:, kc, tsl],
                                                                          start=(kc == 0), stop=(kc == KC - 1)),
                        reads=[Bring[s], Byg[tb]], writes=[BPS[bk[0]]])
                for kc in range(KC):
                    ADD("pe", lambda e, kc=kc, b=bk[1], tsl=tsl, W=W: e.matmul(PS[b][:, :], lhsT=W[:, kc, 128:256], rhs=bcT[:, kc, tsl],
                                                                          start=(kc == 0), stop=(kc == KC - 1)),
                        reads=[Bring[s], Bbc[tb]], writes=[BPS[bk[1]]])
                for kc in range(KC):
                    ADD("pe", lambda e, kc=kc, b=bk[2], tsl=tsl, W=W: e.matmul(PS[b][:, :], lhsT=W[:, kc, 256:384], rhs=hT[:, kc, tsl],
                                                                          start=(kc == 0), stop=(kc == KC - 1)),
                        reads=[Bring[s]] + BhT[tb * 4:(tb + 1) * 4], writes=[BPS[bk[2]]])
                for kc in range(KC):
                    ADD("pe", lambda e, kc=kc, b=bk[3], tsl=tsl, W=W: e.matmul(PS[b][:, :], lhsT=W[:, kc, 384:512], rhs=hT[:, kc, tsl],
                                                                          start=(kc == 0), stop=(kc == KC - 1)),
                        reads=[Bring[s]] + BhT[tb * 4:(tb + 1) * 4], writes=[BPS[bk[3]]])
                sa, sbb = p2t[st]
                ADD("act", lambda e, sa=sa, b=bk[2]: e.activation(out=sa, in_=PS[b][:, :], func=AF.Sigmoid),
                    reads=[BPS[bk[2]]], writes=[Bp2t[st][0]])
                ADD("act", lambda e, sbb=sbb, b=bk[3]: e.activation(out=sbb, in_=PS[b][:, :], func=AF.Sigmoid),
                    reads=[BPS[bk[3]]], writes=[Bp2t[st][1]])
                ADD("dve", lambda e, sa=sa, b=bk[0]: e.tensor_tensor(out=sa, in0=PS[b][:, :], in1=sa, op=ALU.mult),
                    reads=[BPS[bk[0]], Bp2t[st][0]], writes=[Bp2t[st][0]])
                ADD("dve", lambda e, sbb=sbb, b=bk[1]: e.tensor_tensor(out=sbb, in0=PS[b][:, :], in1=sbb, op=ALU.mult),
                    reads=[BPS[bk[1]], Bp2t[st][1]], writes=[Bp2t[st][1]])
                ADD("pool", lambda e, sa=sa, sbb=sbb, dc=dc, tsl=tsl: e.tensor_tensor(out=mergedT[:, dc, tsl], in0=sa, in1=sbb, op=ALU.add),
                    reads=[Bp2t[st][0], Bp2t[st][1]], writes=[Bmg[tb]])

        if stage == 2:
            if dbg:
                dbg_ops.append(ADD("sp", lambda e: e.dma_start(out=dbg_d[:, 0:16384], in_=Eb[:, :]), reads=Bmg, dma_key="dbg"))
            finish()
            return nc

        wload(ring_flat(1), mixl_d[P:2 * P, :], [Bring[1]], "ring1")
        ada_bcast(0, gtm, Bgtm, 2, 0)
        ada_bcast(1, gtm, Bgtm, 3, 1)
        load_ada(10, 2)
        load_ada(11, 3)

        def load_gu(g):
            start, n = FFN_GROUPS[g]
            rb = (g + 1) % 2
            bufs = [Bring[2 * rb], Bring[2 * rb + 1]]
            wload(ring[:, rb * 8192:rb * 8192 + 2048 * n], wgu_d[g][:, :], bufs, "ring%d" % (2 * rb))

        handoff(Byg + Bbc, Bx1)
        Bxs2 = [Buf("xs2_0"), Buf("xs2_1")]
        handoff([Bconst, Badax], Bxs2)
        xs2 = [U[:, 3072:4096], U[:, 4096:5120]]
        xn2 = [U[:, 5120 + i * 1024:5120 + (i + 1) * 1024] for i in range(4)]
        Bxn2 = [Buf("xn2_%d" % i) for i in range(4)]
        handoff(Bp2t[0] + Bp2t[1], Bxn2)

        def p4_A(t):
            norm_A(t, x1[:, t, :], [Bx1[t]], xn2[t % 4], Bxn2[t % 4])

        def p4_B(tb):
            ts_ = range(tb * 4, tb * 4 + 4)
            norm_B_block(tb, [xn2[t % 4] for t in ts_], [Bxn2[t % 4] for t in ts_], c_A2, c_B2, BAB2)

        cnt = 0
        for t in range(NT):
            s2 = t % 2
            ADD("sp", lambda e, t=t, s2=s2: e.dma_start(out=xs2[s2], in_=x_d[t * P:(t + 1) * P, :]), writes=[Bxs2[s2]], dma_key="xs2_%d" % s2)
            for hf in range(2):
                bank = 2 + cnt % 6
                cnt += 1
                hs = slice(hf * 512, (hf + 1) * 512)
                Wm = ring_slot(hf)
                for dc in range(KC):
                    ADD("pe", lambda e, dc=dc, bank=bank, Wm=Wm, t=t: e.matmul(PS[bank][:, :], lhsT=mergedT[:, dc, t * P:(t + 1) * P],
                                                                              rhs=Wm[:, dc, :], start=(dc == 0), stop=(dc == KC - 1)),
                        reads=[Bmg[t // 4], Bring[hf]], writes=[BPS[bank]])
                ADD("dve", lambda e, bank=bank, t=t, hs=hs: e.tensor_tensor(out=x1[:, t, hs], in0=PS[bank][:, :], in1=gtm[:, hs], op=ALU.mult),
                    reads=[BPS[bank], Bgtm], writes=[Bx1[t]])
                ADD("pool", lambda e, t=t, hs=hs, s2=s2: e.tensor_tensor(out=x1[:, t, hs], in0=x1[:, t, hs], in1=xs2[s2][:, hs], op=ALU.add),
                    reads=[Bx1[t], Bxs2[s2]], writes=[Bx1[t]])
            if t >= 5 and (t - 5) % 4 == 0:
                p4_B((t - 5) // 4)
            if t >= 1:
                p4_A(t - 1)
            if t == 3:
                ada_bcast(0, gtf, Bgtf, 2, 0)
                ada_bcast(1, gtf, Bgtf, 3, 1)
                load_gu(0)
        p4_A(NT - 1)
        if stage == 3:
            for t in range(NT):
                final_waits.append(ADD("sp", lambda e, t=t: e.dma_start(out=out_d[t * P:(t + 1) * P, :], in_=x1[:, t, :]),
                                       reads=[Bx1[t]], dma_key="out"))
            finish()
            return nc

        wdv = wd_d.rearrange("(c p) n -> p c n", p=P)
        Wdb = [Ub[:, 10240:14336].rearrange("p (c n) -> p c n", c=4), Ub[:, 14336:18432].rearrange("p (c n) -> p c n", c=4)]
        BWdb = [Buf("wdb0"), Buf("wdb1")]
        stg = [U[:, 3072:4096], U[:, 4096:5120]]
        Bstg = [Buf("stg0"), Buf("stg1")]
        sgf = [U[:, 0:512], U[:, 512:1024]]
        Bsgf = [Buf("sgf0"), Buf("sgf1")]
        actT = [Eb[:, 0:8192].rearrange("p (c s) -> p c s", c=4), Eb[:, 8192:16384].rearrange("p (c s) -> p c s", c=4)]
        Bact = [[Buf("act%d_%d" % (i, tb)) for tb in range(4)] for i in range(2)]

        stg_cnt = [0]

        def load_wd(g):
            start, n = FFN_GROUPS[g]
            for ci in range(n):
                k = stg_cnt[0] % 2
                stg_cnt[0] += 1
                ADD("sp", lambda e, k=k, c=start + ci: e.dma_start(out=stg[k], in_=wdv[:, c, :]), writes=[Bstg[k]], dma_key="stg%d" % k)
                ADD("pool", lambda e, k=k, ci=ci, g=g: e.tensor_tensor(out=Wdb[g % 2][:, ci, :], in0=stg[k], in1=gtf, op=ALU.mult),
                    reads=[Bstg[k], Bgtf], writes=[BWdb[g % 2]])

        handoff(Bxs2, Bstg)
        handoff([Bgtm], Bsgf)
        handoff(Bmg, Bact[0] + Bact[1])
        load_gu(1)

        if stage == 4:
            if dbg:
                dbg_ops.append(ADD("sp", lambda e: e.dma_start(out=dbg_d[:, 0:16384], in_=hT[:, :, :].rearrange("p k s -> p (k s)")),
                                   reads=BhT, dma_key="dbg"))
            finish()
            return nc

        gu_cnt = [0]
        dn_cnt = [0]

        def gu(g, tb):
            start, n = FFN_GROUPS[g]
            b = g % 2
            rb = (g + 1) % 2
            W = ring[:, rb * 8192:rb * 8192 + 2048 * n].rearrange("p (k n) -> p k n", k=KC)
            tsl = slice(tb * 512, (tb + 1) * 512)
            for ci in range(n):
                st = gu_cnt[0] % 2
                gu_cnt[0] += 1
                gb, ub = (0, 1) if st == 0 else (2, 3)
                for kc in range(KC):
                    ADD("pe", lambda e, kc=kc, ci=ci, gb=gb: e.matmul(PS[gb][:, :], lhsT=W[:, kc, ci * P:(ci + 1) * P], rhs=hT[:, kc, tsl],
                                                                      start=(kc == 0), stop=(kc == KC - 1)),
                        reads=[Bring[2 * rb], Bring[2 * rb + 1]] + BhT[tb * 4:(tb + 1) * 4], writes=[BPS[gb]])
                for kc in range(KC):
                    ADD("pe", lambda e, kc=kc, ci=ci, ub=ub, n=n: e.matmul(PS[ub][:, :], lhsT=W[:, kc, (n + ci) * P:(n + ci + 1) * P], rhs=hT[:, kc, tsl],
                                                                      start=(kc == 0), stop=(kc == KC - 1)),
                        reads=[Bring[2 * rb], Bring[2 * rb + 1]] + BhT[tb * 4:(tb + 1) * 4], writes=[BPS[ub]])
                ADD("act", lambda e, st=st, gb=gb: e.activation(out=sgf[st], in_=PS[gb][:, :], func=AF.Silu),
                    reads=[BPS[gb]], writes=[Bsgf[st]])
                ADD("dve", lambda e, st=st, ub=ub, ci=ci: e.tensor_tensor(out=actT[b][:, ci, tsl], in0=PS[ub][:, :], in1=sgf[st], op=ALU.mult),
                    reads=[BPS[ub], Bsgf[st]], writes=[Bact[b][tb]])

        def down(g, tb):
            start, n = FFN_GROUPS[g]
            b = g % 2
            for t in range(tb * 4, tb * 4 + 4):
                for hf in range(2):
                    bank = 4 + dn_cnt[0] % 4
                    dn_cnt[0] += 1
                    hs = slice(hf * 512, (hf + 1) * 512)
                    for ci in range(n):
                        ADD("pe", lambda e, ci=ci, bank=bank, t=t, hs=hs: e.matmul(PS[bank][:, :], lhsT=actT[b][:, ci, t * P:(t + 1) * P],
                                                                                  rhs=Wdb[b][:, ci, hs], start=(ci == 0), stop=(ci == n - 1)),
                            reads=[Bact[b][tb], BWdb[b]], writes=[BPS[bank]])
                    ADD("dve", lambda e, bank=bank, t=t, hs=hs: e.tensor_tensor(out=x1[:, t, hs], in0=PS[bank][:, :], in1=x1[:, t, hs], op=ALU.add),
                        reads=[BPS[bank], Bx1[t]], writes=[Bx1[t]])

        def final_block(tb):
            ts_ = list(range(tb * 4, tb * 4 + 4))
            for t in ts_:
                ADD("act", lambda e, t=t: e.activation(out=stg[t % 2], in_=x1[:, t, :], func=AF.Square, accum_out=c_ss[:, t:t + 1]),
                    reads=[Bx1[t], Bstg[t % 2]], writes=[Bstg[t % 2], Bss[t]])
            for t in ts_:
                ADD("dve", lambda e, t=t: e.tensor_scalar(out=c_tmp[:, t:t + 1], in0=c_ss[:, t:t + 1], scalar1=float(1.0 / D),
                                                          scalar2=float(EPS), op0=ALU.mult, op1=ALU.add), reads=[Bss[t]], writes=[Bss[t]])
            for t in ts_:
                ADD("pool", lambda e, t=t: e.tensor_tensor(out=c_rs[:, t:t + 1], in0=c_tmp[:, t:t + 1], in1=c_mhalf, op=ALU.pow),
                    reads=[Bss[t], Bcols], writes=[Bss[t]])
            for t in ts_:
                ADD("dve", lambda e, t=t: e.scalar_tensor_tensor(out=x1[:, t, :], in0=x1[:, t, :], scalar=c_rs[:, t:t + 1], in1=gfin,
                                                                 op0=ALU.mult, op1=ALU.mult), reads=[Bx1[t], Bss[t], Bgfin], writes=[Bx1[t]])
                final_waits.append(ADD("sp", lambda e, t=t: e.dma_start(out=out_d[t * P:(t + 1) * P, :], in_=x1[:, t, :]),
                                       reads=[Bx1[t]], dma_key="out"))

        def down_merged(gs, tb):
            chunks = [(g_ % 2, ci) for g_ in gs for ci in range(FFN_GROUPS[g_][1])]
            for t in range(tb * 4, tb * 4 + 4):
                for hf in range(2):
                    bank = 4 + dn_cnt[0] % 4
                    dn_cnt[0] += 1
                    hs = slice(hf * 512, (hf + 1) * 512)
                    for k_, (b_, ci) in enumerate(chunks):
                        ADD("pe", lambda e, ci=ci, b_=b_, bank=bank, t=t, hs=hs, k_=k_: e.matmul(
                            PS[bank][:, :], lhsT=actT[b_][:, ci, t * P:(t + 1) * P], rhs=Wdb[b_][:, ci, hs],
                            start=(k_ == 0), stop=(k_ == len(chunks) - 1)),
                            reads=[Bact[0][tb], Bact[1][tb], BWdb[0], BWdb[1]], writes=[BPS[bank]])
                    ADD("dve", lambda e, bank=bank, t=t, hs=hs: e.tensor_tensor(out=x1[:, t, hs], in0=PS[bank][:, :], in1=x1[:, t, hs], op=ALU.add),
                        reads=[BPS[bank], Bx1[t]], writes=[Bx1[t]])

        NG = len(FFN_GROUPS)
        for g in range(NG):
            if g >= 1:
                if g + 1 < NG:
                    load_gu(g + 1)
            for tb in range(4):
                gu(g, tb)
                if g == 0 and tb == 0:
                    p4_B(3)
                    handoff(Bp2t[0] + Bp2t[1] + Bxn2, BWdb)
                    load_wd(0)
                if 1 <= g <= NG - 2:
                    down(g - 1, tb)
            if g + 1 < NG:
                load_wd(g + 1)
        for tb in range(4):
            down_merged([NG - 2, NG - 1], tb)
            final_block(tb)
        finish()
    return nc


def _consts():
    h = np.arange(H, dtype=np.float64)
    log_gamma = np.log(1.0 - 2.0 ** (-5.0 - h))
    idx = np.arange(P, dtype=np.float64)
    dscale = float(P) ** -0.5
    mask = np.zeros((P, H, P), dtype=np.float64)
    for hh in range(H):
        col = dscale * np.exp(-log_gamma[hh] * (idx + 1.0))
        mm = np.where(idx[None, :] >= idx[:, None], 1.0, 0.0)
        mask[:, hh, :] = mm * col[:, None]
    zeta = dscale * np.exp(log_gamma[None, :] * (P - 1 - idx)[:, None])
    xi = np.exp(log_gamma[None, :] * (idx + 1.0)[:, None])
    epsp = EPS / (xi * xi)
    inv_freq = 1.0 / (10000.0 ** (np.arange(0, P, 2, dtype=np.float32) / np.float32(P)))
    invf = np.broadcast_to(inv_freq.astype(np.float32)[None, :], (P, 64))
    return (np.ascontiguousarray(mask.reshape(P, H * P).astype(np.float32)), np.ascontiguousarray(zeta.astype(np.float32)),
            np.ascontiguousarray(epsp.astype(np.float32)), np.ascontiguousarray(invf.astype(np.float32)),
            np.eye(P, dtype=np.float32))


def make_in_maps(x, c, positions, ada_w, ada_b, norm_mix_g, w_in, conv_w, ret_w_out, conv_w_out, mix_w_out,
                 norm_ffn_g, ffn_w_gate, ffn_w_up, ffn_w_down, final_norm_g, n_cores=8):
    f = lambda a: np.ascontiguousarray(np.asarray(a, dtype=np.float32))
    mask, zeta, epsp, invf, ident = _consts()
    col8 = lambda v: np.ascontiguousarray(np.asarray(v, dtype=np.float32).reshape(-1, P).T)
    adaw = f(ada_w[0]); win = f(w_in[0])
    adaw_l = adaw.reshape(8, P, 12, 512).transpose(2, 1, 0, 3).reshape(12 * P, 4096)
    w1_l = win[:, 0:4096].reshape(8, P, 4, 8, P).transpose(3, 1, 0, 2, 4).reshape(8 * P, 4096)
    wc_l = win[:, 4096:7168].reshape(8, P, 3, 8, P).transpose(3, 1, 0, 2, 4).reshape(8 * P, 3072)
    st4 = np.stack([f(ret_w_out[0]), f(conv_w_out[0]), win[:, 7168:8192], win[:, 8192:9216]], axis=1)
    wp2_l = st4.reshape(8, P, 4, 8, P).transpose(3, 1, 0, 2, 4).reshape(8 * P, 4096)
    mix_l = f(mix_w_out[0]).reshape(8, P, 2, 512).transpose(2, 1, 0, 3).reshape(2 * P, 4096)
    wg = f(ffn_w_gate[0]).reshape(8, P, HID); wu = f(ffn_w_up[0]).reshape(8, P, HID)
    shared = {
        "adaw_l": np.ascontiguousarray(adaw_l), "adab_col": col8(ada_b[0]), "adab_row": f(ada_b[0]).reshape(1, -1),
        "gmix_col": col8(norm_mix_g[0]), "gffn_col": col8(norm_ffn_g[0]), "gfin_row": f(final_norm_g).reshape(1, -1),
        "w1_l": np.ascontiguousarray(w1_l), "wc_l": np.ascontiguousarray(wc_l), "wp2_l": np.ascontiguousarray(wp2_l),
        "mix_l": np.ascontiguousarray(mix_l),
        "convw_col": np.ascontiguousarray(f(conv_w[0]).reshape(3, 8, P).transpose(2, 1, 0).reshape(P, 24)),
        "ffn_w_down": f(ffn_w_down[0]),
        "invf": invf, "maskT": mask, "zeta": zeta, "epsp": epsp, "ident_in": ident,
    }
    for g, (st_, n) in enumerate(FFN_GROUPS):
        gg = wg[:, :, st_ * P:(st_ + n) * P]; uu = wu[:, :, st_ * P:(st_ + n) * P]
        shared["wgu_l%d" % g] = np.ascontiguousarray(np.stack([gg, uu], axis=2).transpose(1, 0, 2, 3).reshape(P, 2048 * n))
    maps = []
    xs = np.asarray(x, dtype=np.float32)
    cs = np.asarray(c, dtype=np.float32)
    ps = np.asarray(positions, dtype=np.int32)
    for b in range(n_cores):
        m = dict(shared)
        m["x"] = np.ascontiguousarray(xs[b])
        m["ccol"] = col8(cs[b])
        m["pos"] = np.ascontiguousarray(ps[b].reshape(NT, P).T)
        maps.append(m)
    return maps


_NC_CACHE = {}


def kernel(**inputs):
    if "nc" not in _NC_CACHE:
        _NC_CACHE["nc"] = build_nc()
    nc = _NC_CACHE["nc"]
    in_maps = make_in_maps(**inputs)
    res = run_bass_kernel_spmd(nc, in_maps, core_ids=list(range(8)))
    out = np.stack([np.asarray(r["out"], dtype=np.float32) for r in res.results], axis=0)
    return out
```

```python
import numpy as np
from contextlib import ExitStack
import concourse.bass as bass
import concourse.mybir as mybir
from concourse.bass_utils import run_bass_kernel_spmd

F32 = mybir.dt.float32
BF16 = mybir.dt.bfloat16
I32 = mybir.dt.int32
AF = mybir.ActivationFunctionType
ALU = mybir.AluOpType

P = 128
SEQ = 2048
NT = 16
D = 1024
KC = 8
H = 8
HID = 2816
NHC = 22
EPS = 1e-6
TWO_PI = 2.0 * np.pi

ENGS = ("pe", "act", "dve", "pool", "sp")
import os
CUT = int(os.environ.get("CUT", "1000000000"))
NOSELF = os.environ.get("NOSELF", "")


class Buf:
    __slots__ = ("name", "writers", "readers", "excl", "last")

    def __init__(self, name, excl=False):
        self.name = name
        self.writers = []
        self.readers = []
        self.excl = excl
        self.last = {}


class Op:
    __slots__ = ("eng", "fn", "deps", "idx", "ticket", "dma_key", "dma_val", "needs_inc", "is_dma")


class Sched:
    def __init__(self):
        self.ops = []
        self.q = {e: [] for e in ENGS}
        self.dma_counts = {}

    def add(self, eng, fn, reads=(), writes=(), dma_key=None, extra_deps=()):
        op = Op()
        op.eng = eng
        op.fn = fn
        op.idx = len(self.ops)
        op.is_dma = dma_key is not None
        op.dma_key = dma_key
        op.needs_inc = False
        op.ticket = None
        op.dma_val = None
        deps = []
        for b in reads:
            deps.extend(b.writers)
        for b in writes:
            if b.readers:
                deps.extend(b.readers)
                deps.extend(b.writers)
        deps.extend(extra_deps)
        for b in list(reads) + list(writes):
            if b.excl:
                for e2, o2 in b.last.items():
                    if e2 != eng:
                        deps.append(o2)
        best_e = {}
        best_d = {}
        for d in deps:
            if d is op:
                continue
            if d.is_dma:
                if d.dma_key not in best_d or best_d[d.dma_key].dma_val < d.dma_val:
                    best_d[d.dma_key] = d
            else:
                if d.eng not in best_e or best_e[d.eng].idx < d.idx:
                    best_e[d.eng] = d
        op.deps = list(best_e.values()) + list(best_d.values())
        if op.is_dma:
            self.dma_counts[dma_key] = self.dma_counts.get(dma_key, 0) + 16
            op.dma_val = self.dma_counts[dma_key]
        for b in list(reads) + list(writes):
            if b.excl:
                b.last[eng] = op
        for b in reads:
            b.readers.append(op)
        for b in writes:
            if b.readers:
                b.readers = []
                b.writers = [op]
            else:
                b.writers.append(op)
        self.ops.append(op)
        self.q[eng].append(op)
        return op

    def emit(self, nc, final_waits=()):
        for op in self.ops:
            for d in op.deps:
                if not d.is_dma:
                    d.needs_inc = True
        for e in ENGS:
            t = 0
            for op in self.q[e]:
                if not op.is_dma and op.needs_inc:
                    t += 1
                    op.ticket = t
        with ExitStack() as es:
            esem = {e: es.enter_context(nc.semaphore("s_" + e)) for e in ENGS}
            dsem = {k: es.enter_context(nc.semaphore("d_%s" % (k,))) for k in self.dma_counts}
            block = es.enter_context(nc.Block())

            def make(e):
                def body(engine):
                    seen_e = {x: 0 for x in ENGS}
                    seen_d = {k: 0 for k in self.dma_counts}
                    for op in self.q[e]:
                        if op.idx >= CUT:
                            break
                        for d in op.deps:
                            if d.is_dma:
                                if seen_d[d.dma_key] < d.dma_val:
                                    engine.wait_ge(dsem[d.dma_key], d.dma_val)
                                    seen_d[d.dma_key] = d.dma_val
                            else:
                                if d.eng == e and NOSELF and e in NOSELF.split(","):
                                    continue
                                if seen_e[d.eng] < d.ticket:
                                    engine.wait_ge(esem[d.eng], d.ticket)
                                    seen_e[d.eng] = d.ticket
                        ins = op.fn(engine)
                        if op.is_dma:
                            ins.then_inc(dsem[op.dma_key], 16)
                        elif op.needs_inc:
                            ins.then_inc(esem[e], 1)
                    if e == "sp":
                        fin = {}
                        for d in final_waits:
                            if d.idx >= CUT:
                                continue
                            fin[d.dma_key] = max(fin.get(d.dma_key, 0), d.dma_val)
                        if CUT < 1000000000:
                            for op2 in self.ops:
                                if op2.is_dma and op2.idx < CUT:
                                    fin[op2.dma_key] = max(fin.get(op2.dma_key, 0), op2.dma_val)
                        for k, v in fin.items():
                            if seen_d[k] < v:
                                engine.wait_ge(dsem[k], v)
                                seen_d[k] = v
                return body

            block.tensor(make("pe"))
            block.scalar(make("act"))
            block.vector(make("dve"))
            block.gpsimd(make("pool"))
            block.sync(make("sp"))


def handoff(old_bufs, new_bufs):
    users = []
    for b in old_bufs:
        users.extend(b.readers)
        users.extend(b.writers)
    for b in new_bufs:
        b.readers = list(b.readers) + users


FFN_GROUPS = [(0, 4), (4, 4), (8, 4), (12, 4), (16, 3), (19, 3)]


def build_nc(stage=99, dbg=False):
    nc = bass.Bass("TRN2", target_bir_lowering=False)

    def din(name, shape, dt=F32):
        return nc.dram_tensor(name, shape, dt, kind="ExternalInput").ap()

    x_d = din("x", [SEQ, D])
    ccol_d = din("ccol", [P, KC])
    pos_d = din("pos", [P, NT], I32)
    adaw_d = din("adaw_l", [12 * P, 4096])
    adabc_d = din("adab_col", [P, 48])
    adabr_d = din("adab_row", [1, 6 * D])
    gmix_d = din("gmix_col", [P, KC])
    gffn_d = din("gffn_col", [P, KC])
    gfin_d = din("gfin_row", [1, D])
    w1_d = din("w1_l", [8 * P, 4096])
    wc_d = din("wc_l", [8 * P, 3072])
    wp2_d = din("wp2_l", [8 * P, 4096])
    mixl_d = din("mix_l", [2 * P, 4096])
    convw_d = din("convw_col", [P, 8 * 3])
    wgu_d = [din("wgu_l%d" % g, [P, 2048 * n]) for g, (st_, n) in enumerate(FFN_GROUPS)]
    wd_d = din("ffn_w_down", [HID, D])
    invf_d = din("invf", [P, 64])
    mask_d = din("maskT", [P, H * P])
    zeta_d = din("zeta", [P, H])
    epsp_d = din("epsp", [P, H])
    ident_d = din("ident_in", [P, P])
    out_d = nc.dram_tensor("out", [SEQ, D], F32, kind="ExternalOutput").ap()
    if dbg:
        dbg_d = nc.dram_tensor("dbg", [P, 32768], BF16, kind="ExternalOutput").ap()

    S = Sched()
    global _LAST_SCHED
    _LAST_SCHED = S
    decay = [float(np.exp(np.float64(np.log(1.0 - 2.0 ** (-5.0 - h))) * 128.0)) for h in range(H)]

    with ExitStack() as es:
        def sb(name, shape, dt=F32):
            return es.enter_context(nc.sbuf_tensor(name, shape, dt))

        hT = sb("hT", [P, KC, SEQ], BF16)
        BC = sb("BC", [P, 16384], F32)
        BCb = BC[:, :].bitcast(BF16)
        ygT = BCb[:, 0:16384].rearrange("p (h s) -> p h s", h=KC)
        bcT = BCb[:, 16384:32768].rearrange("p (h s) -> p h s", h=KC)
        x1 = BC[:, :].rearrange("p (t d) -> p t d", t=NT)
        E = sb("E", [P, 8192], F32)
        Eb = E[:, :].bitcast(BF16)
        mergedT = Eb.rearrange("p (h s) -> p h s", h=KC)
        ring = sb("ring", [P, 16384], BF16)
        U = sb("U", [P, 9216], F32)
        Ub = U[:, :].bitcast(BF16)

        ident = sb("ident", [P, P])
        identb = sb("identb", [P, P], BF16)
        gtm = U[:, 0:1024]
        gtf = U[:, 1024:2048]
        gfin = U[:, 2048:3072]
        cols = sb("cols", [P, 256])
        csrep = sb("csrep", [P, KC, P], BF16)
        Rst = sb("Rst", [P, P])
        Rb = sb("Rb", [P, 4, P], BF16)
        posi = sb("posi", [P, NT], I32)
        tmp8 = sb("tmp8", [P, 8])
        sqj = sb("sqj", [P, D], BF16)
        Bsqj = Buf("sqj")
        sqj2 = sb("sqj2", [P, D], BF16)
        Bsqj2 = Buf("sqj2")
        dbgt = sb("dbgt", [P, 128])

        c_ccol = cols[:, 0:8]
        c_cs = cols[:, 8:16]
        c_gmix = cols[:, 16:24]
        c_gffn = cols[:, 24:32]
        c_adab = cols[:, 32:80]
        c_convw = cols[:, 80:104]
        c_zeta = cols[:, 104:112]
        c_epsp = cols[:, 112:120]
        c_A1 = cols[:, 120:128]
        c_B1 = cols[:, 128:136]
        c_A2 = cols[:, 136:144]
        c_B2 = cols[:, 144:152]
        c_mhalf = cols[:, 152:153]
        c_posf = cols[:, 160:176]
        c_ss = cols[:, 176:192]
        c_rs = cols[:, 192:208]
        c_tmp = cols[:, 208:224]
        c_ln = cols[:, 224:256]
        csbf = sb("csbf", [P, KC], BF16)
        lnst = sb("lnst", [P, 8, 8])

        cosT = U[:, 0:1024].rearrange("p (t j) -> p t j", t=NT)
        sinT = U[:, 1024:2048].rearrange("p (t j) -> p t j", t=NT)
        nsinT = U[:, 2048:3072].rearrange("p (t j) -> p t j", t=NT)
        maskT = U[:, 3072:4096].rearrange("p (h c) -> p h c", h=H)
        p1 = BC[:, 0:4096]
        cvt = BC[:, 4096:8192]
        PS = [es.enter_context(nc.psum_tensor("ps%d" % i, [P, 512], F32)) for i in range(8)]
        BPS = [Buf("ps%d" % i, excl=True) for i in range(8)]

        Bconst = Buf("const")
        BhT = [Buf("hT%d" % t) for t in range(NT)]
        Byg = [Buf("yg%d" % i) for i in range(4)]
        Bbc = [Buf("bc%d" % i) for i in range(4)]
        Bx1 = [Buf("x1_%d" % t) for t in range(NT)]
        Bmg = [Buf("mg%d" % i) for i in range(4)]
        Bring = [Buf("ring%d" % i) for i in range(4)]
        Bcols = Buf("cols_setup")
        BAB1 = Buf("AB1")
        BAB2 = Buf("AB2")
        Bgtm = Buf("gtm")
        Bgtf = Buf("gtf")
        Bgfin = Buf("gfin")
        Brope = Buf("rope")
        Bcsrep = Buf("csrep")

        def ring_slot(s):
            return ring[:, s * 4096:(s + 1) * 4096].rearrange("p (k n) -> p k n", k=KC)

        def ring_big(b):
            return ring[:, b * 8192:(b + 1) * 8192].rearrange("p (k n) -> p k n", k=KC)

        def wview(w_ap):
            return w_ap.rearrange("(k p) n -> p k n", p=P)

        ADD = S.add

        cdma_n = [0]

        def cdma(out_ap, in_ap):
            q = "sp" if cdma_n[0] % 2 == 0 else "act"
            cdma_n[0] += 1
            ADD(q, lambda e: e.dma_start(out=out_ap, in_=in_ap), writes=[Bconst], dma_key="const")

        cdma(c_ccol, ccol_d[:, :])
        cdma(posi[:, :], pos_d[:, :])
        cdma(c_adab, adabc_d[:, :])
        cdma(c_gmix, gmix_d[:, :])
        cdma(c_gffn, gffn_d[:, :])
        cdma(c_convw, convw_d[:, :])
        cdma(c_zeta, zeta_d[:, :])
        cdma(c_epsp, epsp_d[:, :])
        cdma(ident[:, :], ident_d[:, :])
        cdma(p1[:, 0:64], invf_d[:, :])
        cdma(maskT, mask_d[:, :].rearrange("p (h c) -> p h c", h=H))

        wload_extra = [()]

        def wload(dst_flat, src_flat, bufs, key):
            L = dst_flat.shape[1]
            nch = (L + 2047) // 2048
            assert L % nch == 0
            d3 = dst_flat.rearrange("p (c n) -> p c n", c=nch)
            s3 = src_flat.rearrange("p (c n) -> p c n", c=nch)
            return ADD("pool", lambda e: e.dma_start(out=d3, in_=s3), writes=bufs, dma_key=key, extra_deps=wload_extra[0])

        def ring_flat(s, L=4096):
            return ring[:, s * 4096:s * 4096 + L]

        def load_ada(blk, slot):
            wload(ring_flat(slot), adaw_d[blk * P:(blk + 1) * P, :], [Bring[slot]], "ring%d" % slot)

        ada_first = []
        for blk in range(4):
            load_ada(blk, blk)
            ada_first.append(S.ops[-1])

        ADD("dve", lambda e: e.tensor_copy(out=identb[:, :], in_=ident[:, :]), reads=[Bconst], writes=[Bcols])
        ADD("dve", lambda e: e.memset(c_mhalf, -0.5), writes=[Bcols])
        ADD("act", lambda e: e.activation(out=c_cs, in_=c_ccol, func=AF.Silu), reads=[Bconst], writes=[Bcols])
        Bcsbf = Buf("csbf")
        ADD("dve", lambda e: e.tensor_copy(out=csbf[:, :], in_=c_cs), reads=[Bcols], writes=[Bcsbf])
        ADD("dve", lambda e: e.tensor_copy(out=csrep[:, :, :], in_=c_cs.unsqueeze(2).broadcast_to([P, KC, P])),
            reads=[Bcols], writes=[Bcsrep])
        ADD("dve", lambda e: e.tensor_copy(out=c_posf, in_=posi[:, :]), reads=[Bconst], writes=[Bcols])

        ang = p1[:, 64:1088].rearrange("p (t j) -> p t j", t=NT)
        uu = p1[:, 1088:2112].rearrange("p (t j) -> p t j", t=NT)
        ki = cvt[:, 0:1024].bitcast(I32).rearrange("p (t j) -> p t j", t=NT)
        ff = cvt[:, 1024:2048].rearrange("p (t j) -> p t j", t=NT)
        invf = p1[:, 0:64]
        ADD("dve", lambda e: e.memset(Rb[:, :, :], 0.0), writes=[Bcols])

        def ada_cols(blk, ps_cols, slot):
            for j in range(4):
                for kc in range(KC):
                    ADD("pe", lambda e, j=j, kc=kc: e.matmul(PS[4][:, ps_cols + j:ps_cols + j + 1],
                                                             lhsT=ring_slot(slot)[:, kc, j * P:(j + 1) * P],
                                                             rhs=csbf[:, kc:kc + 1], start=(kc == 0), stop=(kc == KC - 1)),
                        reads=[Bring[slot], Bcsbf], writes=[BPS[4]])

        def ada_bcast(blk_half, dst, dstbuf, slot, bank):
            for kc in range(KC):
                ADD("pe", lambda e, kc=kc: e.matmul(PS[bank][:, :], lhsT=csrep[:, kc, :], rhs=ring_slot(slot)[:, kc, :],
                                                    start=(kc == 0), stop=(kc == KC - 1)),
                    reads=[Bring[slot], Bcsrep], writes=[BPS[bank]])
            sl = slice(blk_half * 512, (blk_half + 1) * 512)
            ADD("dve", lambda e: e.tensor_tensor(out=dst[:, sl], in0=PS[bank][:, :], in1=dst[:, sl], op=ALU.add),
                reads=[BPS[bank], dstbuf], writes=[dstbuf])

        for blk in range(4):
            ada_cols(blk, blk * 4, blk)
        ADD("dve", lambda e: e.tensor_tensor(out=c_B1, in0=PS[4][:, 0:8], in1=c_adab[:, 0:8], op=ALU.add),
            reads=[BPS[4], Bconst], writes=[BAB1])
        ADD("dve", lambda e: e.tensor_tensor(out=tmp8[:, :], in0=PS[4][:, 8:16], in1=c_adab[:, 8:16], op=ALU.add),
            reads=[BPS[4], Bconst], writes=[BAB1])
        ADD("dve", lambda e: e.scalar_tensor_tensor(out=c_A1, in0=tmp8[:, :], scalar=1.0, in1=c_gmix,
                                                    op0=ALU.add, op1=ALU.mult), reads=[BAB1, Bconst], writes=[BAB1])

        ADD("dve", lambda e: e.tensor_tensor(out=ang, in0=c_posf.unsqueeze(2).broadcast_to([P, NT, 64]),
                                              in1=invf.unsqueeze(1).broadcast_to([P, NT, 64]), op=ALU.mult),
            reads=[Bcols, Bconst], writes=[Brope])
        for (dst, shift) in ((sinT, 0.5), (cosT, 0.75)):
            ADD("dve", lambda e, shift=shift: e.tensor_scalar(out=uu, in0=ang, scalar1=float(1.0 / TWO_PI), scalar2=shift,
                                                               op0=ALU.mult, op1=ALU.add), reads=[Brope], writes=[Brope])
            ADD("dve", lambda e: e.tensor_copy(out=ki, in_=uu), reads=[Brope], writes=[Brope])
            ADD("dve", lambda e: e.tensor_tensor(out=ff, in0=uu, in1=ki, op=ALU.subtract), reads=[Brope], writes=[Brope])
            ADD("dve", lambda e: e.tensor_scalar(out=uu, in0=ff, scalar1=0.0, scalar2=None, op0=ALU.is_lt),
                reads=[Brope], writes=[Brope])
            ADD("dve", lambda e: e.tensor_tensor(out=ff, in0=ff, in1=uu, op=ALU.add), reads=[Brope], writes=[Brope])
            ADD("dve", lambda e: e.tensor_scalar(out=ff, in0=ff, scalar1=1.0, scalar2=0.0, op0=ALU.min, op1=ALU.max),
                reads=[Brope], writes=[Brope])
            ADD("act", lambda e, dst=dst: e.activation(out=dst, in_=ff, func=AF.Sin, bias=float(-np.pi), scale=float(TWO_PI)),
                reads=[Brope, Bcols], writes=[Brope])
        ADD("dve", lambda e: e.tensor_scalar(out=nsinT, in0=sinT, scalar1=-1.0, scalar2=None, op0=ALU.mult),
            reads=[Brope], writes=[Brope])

        tp_par = [0]

        def norm_A(t, src_ap, src_bufs, xn_ap, xn_buf, inplace=False):
            if inplace:
                ADD("dve", lambda e: e.scalar_tensor_tensor(out=sqj[:, :], in0=src_ap, scalar=1.0, in1=src_ap, op0=ALU.mult,
                                                            op1=ALU.mult, accum_out=c_ss[:, t:t + 1]),
                    reads=src_bufs + [Bsqj], writes=[Bsqj, Bss[t]])
            else:
                ADD("dve", lambda e: e.scalar_tensor_tensor(out=xn_ap, in0=src_ap, scalar=1.0, in1=src_ap, op0=ALU.mult,
                                                            op1=ALU.mult, accum_out=c_ss[:, t:t + 1]),
                    reads=src_bufs, writes=[xn_buf, Bss[t]])
            ADD("dve", lambda e: e.tensor_scalar(out=c_tmp[:, t:t + 1], in0=c_ss[:, t:t + 1], scalar1=float(1.0 / D),
                                                 scalar2=float(EPS), op0=ALU.mult, op1=ALU.add),
                reads=[Bss[t]], writes=[Bss[t]])
            ADD("pool", lambda e: e.tensor_tensor(out=c_rs[:, t:t + 1], in0=c_tmp[:, t:t + 1], in1=c_mhalf, op=ALU.pow),
                reads=[Bss[t], Bcols], writes=[Bss[t]])
            def stage2():
                ADD("act", lambda e: e.activation(out=xn_ap, in_=src_ap, func=AF.Identity, scale=c_rs[:, t:t + 1]),
                    reads=src_bufs + [Bss[t]], writes=[xn_buf])
            norm_pend.append(stage2)
            while len(norm_pend) > 1:
                norm_pend.pop(0)()

        norm_pend = []

        def norm_flush():
            while norm_pend:
                norm_pend.pop(0)()

        def norm_B_block(tb, xn_aps, xn_bufs, Acol, Bcol, ABbuf):
            for half in range(2):
                for kq in range(4):
                    kc = half * 4 + kq
                    bank = kc
                    for i in range(4):
                        ADD("pe", lambda e, kc=kc, bank=bank, i=i: e.transpose(out=PS[bank][:, i * P:(i + 1) * P],
                                                                               in_=xn_aps[i][:, kc * P:(kc + 1) * P], identity=ident[:, :]),
                            reads=[xn_bufs[i], Bconst], writes=[BPS[bank]])
                for kq in range(4):
                    kc = half * 4 + kq
                    bank = kc
                    dst = hT[:, kc, tb * 512:(tb + 1) * 512]
                    if half == 0:
                        ADD("act", lambda e, bank=bank, dst=dst, kc=kc: e.activation(out=dst, in_=PS[bank][:, :], func=AF.Identity,
                                                                                     bias=Bcol[:, kc:kc + 1], scale=Acol[:, kc:kc + 1]),
                            reads=[BPS[bank], ABbuf], writes=BhT[tb * 4:(tb + 1) * 4])
                    else:
                        ADD("dve", lambda e, bank=bank, dst=dst, kc=kc: e.tensor_scalar(out=dst, in0=PS[bank][:, :], scalar1=Acol[:, kc:kc + 1],
                                                                                        scalar2=Bcol[:, kc:kc + 1], op0=ALU.mult, op1=ALU.add),
                            reads=[BPS[bank], ABbuf], writes=BhT[tb * 4:(tb + 1) * 4])

        def norm_schedule(A_fn, B_fn, R=6):
            done_B = -1
            nextA = 0
            for tb in range(4):
                while nextA < NT and (nextA - R) // 4 <= done_B and nextA < 4 * tb + R:
                    A_fn(nextA)
                    nextA += 1
                norm_flush()
                B_fn(tb)
                done_B = tb
            assert nextA == NT

        Bss = [Buf("ss%d" % t) for t in range(NT)]

        Bxs = []
        Bxn = [Buf("xn%d" % i) for i in range(8)]
        xn_slot = [E[:, i * 1024:(i + 1) * 1024] for i in range(8)]

        def p0_A(t):
            s_ = t % 8
            ADD("sp", lambda e, s_=s_, t=t: e.dma_start(out=xn_slot[s_], in_=x_d[t * P:(t + 1) * P, :]),
                writes=[Bxn[s_]], dma_key="xn%d" % s_, extra_deps=(ada_first if t >= 2 else ()))
            norm_A(t, xn_slot[s_], [Bxn[s_]], xn_slot[s_], Bxn[s_], inplace=True)

        def p0_B(tb):
            ts_ = range(tb * 4, tb * 4 + 4)
            norm_B_block(tb, [xn_slot[t % 8] for t in ts_], [Bxn[t % 8] for t in ts_], c_A1, c_B1, BAB1)

        norm_schedule(p0_A, p0_B, R=8)

        final_waits = []
        dbg_ops = []

        def finish():
            S.emit(nc, final_waits=final_waits + dbg_ops)

        if stage == 0:
            if dbg:
                dbg_ops.append(ADD("sp", lambda e: e.dma_start(out=dbg_d[:, 0:16384], in_=hT[:, :, :].rearrange("p k s -> p (k s)")),
                                   reads=BhT, dma_key="dbg"))
            finish()
            return nc

        handoff([Brope] + Bxs + Bxn, [])
        def load_qkvg(h):
            s_ = h % 2
            wload(ring_flat(s_), w1_d[h * P:(h + 1) * P, :], [Bring[s_]], "ring%d" % s_)

        def load_conv(g):
            cs_ = 2 + g % 2
            wload(ring_flat(cs_, 3072), wc_d[g * P:(g + 1) * P, :], [Bring[cs_]], "ring%d" % cs_)

        load_qkvg(0)
        load_conv(0)
        load_qkvg(1)

        def mk(off, width, depth, dt=F32):
            aps = []
            for i in range(depth):
                a = E[:, off + i * width: off + (i + 1) * width]
                if dt == BF16:
                    a = a.bitcast(BF16)
                aps.append(a)
            return aps, off + depth * width
        o = 0
        tA, o = mk(o, 256, 2)
        tB, o = mk(o, 256, 2)
        tQK, o = mk(o, 128, 4, BF16)
        tvb, o = mk(o, 64, 4, BF16)
        tvz, o = mk(o, 64, 4, BF16)
        tsg, o = mk(o, 128, 8)
        tqkT, o = mk(o, 128, 4, BF16)
        tsm, o = mk(o, 64, 4, BF16)
        ty, o = mk(o, 128, 4)
        tyg, o = mk(o, 64, 4, BF16)
        assert o <= 5120
        cv_c = E[:, 5120:5632]
        cv_u = [E[:, 5632:6148], E[:, 6148:6664]]
        cv_t = E[:, 6664:7176]
        cv_b = E[:, 7176:7688]
        BtA = [Buf("tA%d" % i) for i in range(2)]
        BtB = [Buf("tB%d" % i) for i in range(2)]
        BtQK = [Buf("tQK%d" % i) for i in range(4)]
        Btvb = [Buf("tvb%d" % i) for i in range(4)]
        Btvz = [Buf("tvz%d" % i) for i in range(4)]
        Btsg = [Buf("tsg%d" % i) for i in range(8)]
        BtqkT = [Buf("tqkT%d" % i) for i in range(4)]
        Btsm = [Buf("tsm%d" % i) for i in range(4)]
        Bty = [Buf("ty%d" % i) for i in range(4)]
        Btyg = [Buf("tyg%d" % i) for i in range(4)]
        Bln = [Buf("ln%d" % i) for i in range(8)]
        Bcvc, Bcvt, Bcvb = Buf("cvc"), Buf("cvt"), Buf("cvb")
        Bcvu = [Buf("cvu0"), Buf("cvu1")]
        BRst = Buf("Rst")
        BRb = [Buf("Rb%d" % i) for i in range(4)]
        e_p1 = BtA + BtB + BtQK + Btvb + Btvz + Btsg + BtqkT + Btsm + Bty + Btyg + [Bcvc, Bcvt, Bcvb] + Bcvu
        handoff(Bxs + Bxn, e_p1)
        handoff([Brope], Byg + Bbc)
        qkT_ps = [PS[2][:, 0:128].bitcast(BF16), PS[2][:, 128:256].bitcast(BF16)]
        ygps = [PS[2][:, 256 + i * 64:256 + (i + 1) * 64].bitcast(BF16) for i in range(4)]
        sc_ps = [PS[3][:, 0:128], PS[3][:, 256:384]]
        kv_ps = [PS[3][:, 128:256], PS[3][:, 384:512]]
        out_ps = [PS[4][:, 0:128], PS[4][:, 128:256]]
        Bqkps = [BPS[2], BPS[2]]
        Bygps = [BPS[2]] * 4
        Bscps = [BPS[3], BPS[3]]
        Bkvps = [BPS[3], BPS[3]]
        Boutps = [BPS[4], BPS[4]]
        Badaps = BPS[4]

        def S1(G, mid=None):
            h, t = divmod(G, NT)
            s = h % 2
            W = ring_slot(s)
            for half in range(2):
                for kc in range(KC):
                    ADD("pe", lambda e, kc=kc, half=half: e.matmul(PS[half][:, 0:256], lhsT=hT[:, kc, t * P:(t + 1) * P],
                                                                   rhs=W[:, kc, half * 256:(half + 1) * 256],
                                                                   start=(kc == 0), stop=(kc == KC - 1)),
                        reads=[BhT[t], Bring[s]], writes=[BPS[half]])
                if half == 0 and mid is not None:
                    mid()

        def S2(G):
            h, t = divmod(G, NT)
            pqa = PS[0]
            pqb = PS[1]
            T4 = pqa[:, 0:256].rearrange("p (a b j) -> p a b j", a=2, b=2)
            A4 = tA[G % 2].rearrange("p (a b j) -> p a b j", a=2, b=2)
            B4 = tB[G % 2].rearrange("p (a b j) -> p a b j", a=2, b=2)
            CC = cosT[:, t, :].unsqueeze(1).unsqueeze(1).broadcast_to([P, 2, 2, 64])
            SN = sinT[:, t, :].unsqueeze(1).broadcast_to([P, 2, 64])
            NS = nsinT[:, t, :].unsqueeze(1).broadcast_to([P, 2, 64])
            ADD("dve", lambda e: e.tensor_tensor(out=A4, in0=T4, in1=CC, op=ALU.mult),
                reads=[BPS[0], Brope], writes=[BtA[G % 2]])
            ADD("dve", lambda e: e.tensor_tensor(out=B4[:, :, 0, :], in0=T4[:, :, 1, :], in1=NS, op=ALU.mult),
                reads=[BPS[0], Brope], writes=[BtB[G % 2]])
            ADD("dve", lambda e: e.tensor_tensor(out=B4[:, :, 1, :], in0=T4[:, :, 0, :], in1=SN, op=ALU.mult),
                reads=[BPS[0], Brope], writes=[BtB[G % 2]])
            ADD("pool", lambda e: e.tensor_tensor(out=tQK[G % 4], in0=tA[G % 2], in1=tB[G % 2], op=ALU.add),
                reads=[BtA[G % 2], BtB[G % 2]], writes=[BtQK[G % 4]])
            ADD("act", lambda e: e.activation(out=tvb[G % 4], in_=pqb[:, 0:128], func=AF.Identity),
                reads=[BPS[1]], writes=[Btvb[G % 4]])
            ADD("act", lambda e: e.activation(out=tvz[G % 4], in_=pqb[:, 0:128], func=AF.Identity, scale=c_zeta[:, h:h + 1]),
                reads=[BPS[1], Bconst], writes=[Btvz[G % 4]])
            ADD("act", lambda e: e.activation(out=tsg[G % 8], in_=pqb[:, 128:256], func=AF.Silu),
                reads=[BPS[1]], writes=[Btsg[G % 8]])

        def S3(G):
            for j in range(2):
                ADD("pe", lambda e, j=j: e.transpose(out=qkT_ps[G % 2][:, j * P:(j + 1) * P], in_=tQK[G % 4][:, j * P:(j + 1) * P],
                                                     identity=identb[:, :]),
                    reads=[BtQK[G % 4], Bcols], writes=[Bqkps[G % 2]])
            ADD("act", lambda e: e.activation(out=tqkT[G % 4], in_=qkT_ps[G % 2], func=AF.Identity),
                reads=[Bqkps[G % 2]], writes=[BtqkT[G % 4]])

        def S5(G):
            h, t = divmod(G, NT)
            qk = tqkT[G % 4]
            ADD("pe", lambda e: e.matmul(sc_ps[G % 2], lhsT=qk[:, 128:256], rhs=qk[:, 0:128], start=True, stop=True),
                reads=[BtqkT[G % 4]], writes=[Bscps[G % 2]])
            ADD("pe", lambda e: e.matmul(kv_ps[G % 2], lhsT=tQK[G % 4][:, 128:256], rhs=tvz[G % 4], start=True, stop=True),
                reads=[BtQK[G % 4], Btvz[G % 4]], writes=[Bkvps[G % 2]])
            ADD("dve", lambda e: e.tensor_tensor(out=tsm[G % 4], in0=sc_ps[G % 2], in1=maskT[:, h, :], op=ALU.mult),
                reads=[Bscps[G % 2], Bconst], writes=[Btsm[G % 4]])
            if t == 0:
                ADD("dve", lambda e: e.tensor_copy(out=Rst[:, :], in_=kv_ps[G % 2]), reads=[Bkvps[G % 2]], writes=[BRst])
            elif t < NT - 1:
                ADD("dve", lambda e: e.scalar_tensor_tensor(out=Rst[:, :], in0=Rst[:, :], scalar=decay[h], in1=kv_ps[G % 2],
                                                            op0=ALU.mult, op1=ALU.add),
                    reads=[Bkvps[G % 2], BRst], writes=[BRst])
            if t < NT - 1:
                ADD("pool", lambda e: e.tensor_copy(out=Rb[:, (G + 1) % 4, :], in_=Rst[:, :]), reads=[BRst], writes=[BRb[(G + 1) % 4]])

        def S7(G):
            h, t = divmod(G, NT)
            ADD("pe", lambda e: e.matmul(out_ps[G % 2], lhsT=tsm[G % 4], rhs=tvb[G % 4], start=True, stop=(t == 0)),
                reads=[Btsm[G % 4], Btvb[G % 4]], writes=[Boutps[G % 2]])
            if t > 0:
                ADD("pe", lambda e: e.matmul(out_ps[G % 2], lhsT=tqkT[G % 4][:, 0:128], rhs=Rb[:, G % 4, :], start=False, stop=True),
                    reads=[BtqkT[G % 4], BRb[G % 4]], writes=[Boutps[G % 2]])
            sl = G % 8
            st = lnst[:, sl, 0:6]
            mv = lnst[:, sl, 6:8]
            lc = c_ln[:, sl * 4:(sl + 1) * 4]
            ADD("dve", lambda e: e.bn_stats(out=st, in_=out_ps[G % 2]), reads=[Boutps[G % 2]], writes=[Bln[sl]])
            ADD("dve", lambda e: e.bn_aggr(out=mv, in_=st), reads=[Bln[sl]], writes=[Bln[sl]])
            ADD("pool", lambda e: e.tensor_tensor(out=lc[:, 0:1], in0=mv[:, 1:2], in1=c_epsp[:, h:h + 1], op=ALU.add),
                reads=[Bln[sl], Bconst], writes=[Bln[sl]])
            ADD("pool", lambda e: e.tensor_tensor(out=lc[:, 1:2], in0=lc[:, 0:1], in1=c_mhalf, op=ALU.pow),
                reads=[Bln[sl], Bcols], writes=[Bln[sl]])

        def S8a(G):
            sl = G % 8
            mv = lnst[:, sl, 6:8]
            lc = c_ln[:, sl * 4:(sl + 1) * 4]
            ADD("dve", lambda e: e.tensor_scalar(out=ty[G % 4], in0=out_ps[G % 2], scalar1=mv[:, 0:1], scalar2=lc[:, 1:2],
                                                 op0=ALU.subtract, op1=ALU.mult),
                reads=[Boutps[G % 2], Bln[sl]], writes=[Bty[G % 4]])

        def S8b(G):
            ADD("pool", lambda e: e.tensor_tensor(out=tyg[G % 4], in0=ty[G % 4], in1=tsg[G % 8], op=ALU.mult),
                reads=[Bty[G % 4], Btsg[G % 8]], writes=[Btyg[G % 4]])

        def S9(G):
            h, t = divmod(G, NT)
            ADD("pe", lambda e: e.transpose(out=ygps[G % 4], in_=tyg[G % 4], identity=identb[:, :]),
                reads=[Btyg[G % 4], Bcols], writes=[Bygps[G % 4]])
            ADD("act", lambda e: e.activation(out=ygT[:, h, t * P:(t + 1) * P], in_=ygps[G % 4], func=AF.Identity),
                reads=[Bygps[G % 4]], writes=[Byg[t // 4]])

        def conv_slot(g):
            return 2 + g % 2

        def conv_pe(g, tb, part):
            cs_ = conv_slot(g)
            W = ring_flat(cs_, 3072).rearrange("p (k n) -> p k n", k=KC)
            seq = [(j, bank, kc) for j, bank in ((1, 6), (2, 7), (0, 5)) for kc in range(KC)]
            for j, bank, kc in seq[part * 6:(part + 1) * 6]:
                ADD("pe", lambda e, j=j, bank=bank, kc=kc: e.matmul(PS[bank][:, :], lhsT=W[:, kc, j * P:(j + 1) * P],
                                                                    rhs=hT[:, kc, tb * 512:(tb + 1) * 512],
                                                                    start=(kc == 0), stop=(kc == KC - 1)),
                    reads=[Bring[cs_]] + BhT[tb * 4:(tb + 1) * 4], writes=[BPS[bank]])

        def conv_ew_chunks(g, tb):
            cur, prv = tb % 2, (tb + 1) % 2
            w0 = c_convw[:, g * 3 + 0:g * 3 + 1]
            w1 = c_convw[:, g * 3 + 1:g * 3 + 2]
            w2 = c_convw[:, g * 3 + 2:g * 3 + 3]

            def chA():
                ADD("act", lambda e: e.activation(out=cv_c, in_=PS[6][:, :], func=AF.Identity), reads=[BPS[6]], writes=[Bcvc])
                if tb == 0:
                    ADD("pool", lambda e: e.memset(cv_u[cur][:, 0:2], 0.0), writes=[Bcvu[cur]])
                else:
                    ADD("pool", lambda e: e.tensor_copy(out=cv_u[cur][:, 0:2], in_=cv_u[prv][:, 512:514]),
                        reads=[Bcvu[prv]], writes=[Bcvu[cur]])
                ADD("dve", lambda e: e.tensor_tensor(out=cv_u[cur][:, 2:514], in0=PS[7][:, :], in1=cv_c, op=ALU.mult),
                    reads=[BPS[7], Bcvc], writes=[Bcvu[cur]])

            def chB():
                ADD("act", lambda e: e.activation(out=cv_b, in_=PS[5][:, :], func=AF.Identity), reads=[BPS[5]], writes=[Bcvb])
                ADD("act", lambda e: e.activation(out=cv_t, in_=cv_u[cur][:, 2:514], func=AF.Identity, scale=w2),
                    reads=[Bcvu[cur], Bconst], writes=[Bcvt])

            def chC():
                ADD("dve", lambda e: e.scalar_tensor_tensor(out=cv_t, in0=cv_u[cur][:, 1:513], scalar=w1, in1=cv_t,
                                                            op0=ALU.mult, op1=ALU.add),
                    reads=[Bcvu[cur], Bcvt, Bconst], writes=[Bcvt])
                ADD("dve", lambda e: e.scalar_tensor_tensor(out=cv_t, in0=cv_u[cur][:, 0:512], scalar=w0, in1=cv_t,
                                                            op0=ALU.mult, op1=ALU.add),
                    reads=[Bcvu[cur], Bcvt, Bconst], writes=[Bcvt])

            def chD():
                ADD("pool", lambda e: e.tensor_tensor(out=bcT[:, g, tb * 512:(tb + 1) * 512], in0=cv_b, in1=cv_t, op=ALU.mult),
                    reads=[Bcvb, Bcvt], writes=[Bbc[tb]])
            return [chA, chB, chC, chD]

        P2SLOT = [2, 3, 0, 1, 2, 3, 0, 1]

        def load_p2(dc):
            s_ = P2SLOT[dc]
            wload(ring_flat(s_), wp2_d[dc * P:(dc + 1) * P, :], [Bring[s_]], "ring%d" % s_)

        adax = Ub[:, 8192:12288]
        Badax = Buf("adax")

        def ada_f_block(i_):
            blk = 6 + i_
            wload(adax, adaw_d[blk * P:(blk + 1) * P, :], [Badax], "adax")

        def ada_f_mms(i_):
            W_ = adax.rearrange("p (k n) -> p k n", k=KC)
            for jj in range(4):
                for kc in range(KC):
                    ADD("pe", lambda e, jj=jj, kc=kc: e.matmul(PS[4][:, 256 + i_ * 4 + jj:256 + i_ * 4 + jj + 1],
                                                               lhsT=W_[:, kc, jj * P:(jj + 1) * P],
                                                               rhs=csbf[:, kc:kc + 1], start=(kc == 0), stop=(kc == KC - 1)),
                        reads=[Badax, Bcsbf], writes=[BPS[4]])

        NG1 = H * NT
        pending = {}
        for j in range(NG1 + 8):
            if j < NG1:
                h, t = divmod(j, NT)
                S1(j)
                S2(j)
            if 0 <= j - 7 < NG1:
                S9(j - 7)
            if j < NG1:
                conv_pe(h, t // 4, t % 4)
                if t % 4 == 3:
                    for ci_, ch in enumerate(conv_ew_chunks(h, t // 4)):
                        pending.setdefault(j + ci_, []).append(ch)
                if t == 4 and h + 1 < H:
                    if h >= 1:
                        load_qkvg(h + 1)
                    load_conv(h + 1)
                if h < 4 and t == 2:
                    ada_f_block(h)
                if h < 4 and t == 10:
                    ada_f_mms(h)
                if h == 7 and t == 8:
                    load_p2(0)
                if h == 7 and t == 12:
                    load_p2(2)
            if 0 <= j - 1 < NG1:
                S3(j - 1)
            if 0 <= j - 2 < NG1:
                S5(j - 2)
            if 0 <= j - 3 < NG1:
                S7(j - 3)
            if 0 <= j - 4 < NG1:
                S8a(j - 4)
            if 0 <= j - 5 < NG1:
                S8b(j - 5)
            for ch in pending.pop(j, []):
                ch()
        assert not pending
        ADD("dve", lambda e: e.tensor_tensor(out=c_B2, in0=PS[4][:, 256:264], in1=c_adab[:, 24:32], op=ALU.add),
            reads=[BPS[4], Bconst], writes=[BAB2])
        ADD("dve", lambda e: e.tensor_tensor(out=tmp8[:, :], in0=PS[4][:, 264:272], in1=c_adab[:, 32:40], op=ALU.add),
            reads=[BPS[4], Bconst, BAB1], writes=[BAB2])
        ADD("dve", lambda e: e.scalar_tensor_tensor(out=c_A2, in0=tmp8[:, :], scalar=1.0, in1=c_gffn,
                                                    op0=ALU.add, op1=ALU.mult), reads=[BAB2, Bconst], writes=[BAB2])
        if stage == 1:
            if dbg:
                dbg_ops.append(ADD("sp", lambda e: e.dma_start(out=dbg_d[:, 0:32768], in_=BCb[:, :]), reads=Byg + Bbc, dma_key="dbg"))
            finish()
            return nc

        handoff(e_p1, Bmg)
        p2t = [[U[:, 5120 + (2 * s + i) * 512: 5120 + (2 * s + i + 1) * 512] for i in range(2)] for s in range(2)]
        Bp2t = [[Buf("p2t%d%d" % (s, i)) for i in range(2)] for s in range(2)]
        handoff([Brope], [Bgtm, Bgtf, Bgfin])
        ADD("sp", lambda e: e.dma_start(out=gtm, in_=adabr_d[0:1, 2 * D:3 * D].broadcast_to([P, D])), writes=[Bgtm], dma_key="gtm")
        ADD("sp", lambda e: e.dma_start(out=gtf, in_=adabr_d[0:1, 5 * D:6 * D].broadcast_to([P, D])), writes=[Bgtf], dma_key="gtf")
        ADD("sp", lambda e: e.dma_start(out=gfin, in_=gfin_d[0:1, :].broadcast_to([P, D])), writes=[Bgfin], dma_key="gfin")
        cnt = 0
        load_p2(1)
        for dc in range(KC):
            if dc >= 1 and dc + 2 < KC:
                load_p2(dc + 2)
            if dc == 5:
                load_ada(4, 2)
            if dc == 6:
                load_ada(5, 3)
            if dc == 7:
                wload(ring_flat(0), mixl_d[0:P, :], [Bring[0]], "ring0")
            s = P2SLOT[dc]
            W = ring_slot(s)
            for tb in range(4):
                st = cnt % 2
                cnt += 1
                bk = [0, 1, 2, 3] if st == 0 else [4, 5, 6, 7]
                tsl = slice(tb * 512, (tb + 1) * 512)
                for kc in range(KC):
                    ADD("pe", lambda e, kc=kc, b=bk[0], tsl=tsl, W=W: e.matmul(PS[b][:, :], lhsT=W[:, kc, 0:128], rhs=ygT[:, kc, tsl],
                                                                          start=(kc == 0), stop=(kc == KC - 1)),
                        reads=[Bring[s], Byg[tb]], writes=[BPS[bk[0]]])
                for kc in range(KC):
                    ADD("pe", lambda e, kc=kc, b=bk[1], tsl=tsl, W=W: e.matmul(PS[b][:, :], lhsT=W[:, kc, 128:256], rhs=bcT[:, kc, tsl],
                                                                          start=(kc == 0), stop=(kc == KC - 1)),
                        reads=[Bring[s], Bbc[tb]], writes=[BPS[bk[1]]])
                for kc in range(KC):
                    ADD("pe", lambda e, kc=kc, b=bk[2], tsl=tsl, W=W: e.matmul(PS[b][:, :], lhsT=W[:, kc, 256:384], rhs=hT[:, kc, tsl],
                                                                          start=(kc == 0), stop=(kc == KC - 1)),
                        reads=[Bring[s]] + BhT[tb * 4:(tb + 1) * 4], writes=[BPS[bk[2]]])
                for kc in range(KC):
                    ADD("pe", lambda e, kc=kc, b=bk[3], tsl=tsl, W=W: e.matmul(PS[b][:, :], lhsT=W[:, kc, 384:512], rhs=hT[:, kc, tsl],
                                                                          start=(kc == 0), stop=(kc == KC - 1)),
                        reads=[Bring[s]] + BhT[tb * 4:(tb + 1) * 4], writes=[BPS[bk[3]]])
                sa, sbb = p2t[st]
                ADD("act", lambda e, sa=sa, b=bk[2]: e.activation(out=sa, in_=PS[b][:, :], func=AF.Sigmoid),
                    reads=[BPS[bk[2]]], writes=[Bp2t[st][0]])
                ADD("act", lambda e, sbb=sbb, b=bk[3]: e.activation(out=sbb, in_=PS[b][:, :], func=AF.Sigmoid),
                    reads=[BPS[bk[3]]], writes=[Bp2t[st][1]])
                ADD("dve", lambda e, sa=sa, b=bk[0]: e.tensor_tensor(out=sa, in0=PS[b][:, :], in1=sa, op=ALU.mult),
                    reads=[BPS[bk[0]], Bp2t[st][0]], writes=[Bp2t[st][0]])
                ADD("dve", lambda e, sbb=sbb, b=bk[1]: e.tensor_tensor(out=sbb, in0=PS[b][:, :], in1=sbb, op=ALU.mult),
                    reads=[BPS[bk[1]], Bp2t[st][1]], writes=[Bp2t[st][1]])
                ADD("pool", lambda e, sa=sa, sbb=sbb, dc=dc, tsl=tsl: e.tensor_tensor(out=mergedT[:, dc, tsl], in0=sa, in1=sbb, op=ALU.add),
                    reads=[Bp2t[st][0], Bp2t[st][1]], writes=[Bmg[tb]])

        if stage == 2:
            if dbg:
                dbg_ops.append(ADD("sp", lambda e: e.dma_start(out=dbg_d[:, 0:16384], in_=Eb[:, :]), reads=Bmg, dma_key="dbg"))
            finish()
            return nc

        wload(ring_flat(1), mixl_d[P:2 * P, :], [Bring[1]], "ring1")
        ada_bcast(0, gtm, Bgtm, 2, 0)
        ada_bcast(1, gtm, Bgtm, 3, 1)
        load_ada(10, 2)
        load_ada(11, 3)

        def load_gu(g):
            start, n = FFN_GROUPS[g]
            rb = (g + 1) % 2
            bufs = [Bring[2 * rb], Bring[2 * rb + 1]]
            wload(ring[:, rb * 8192:rb * 8192 + 2048 * n], wgu_d[g][:, :], bufs, "ring%d" % (2 * rb))

        handoff(Byg + Bbc, Bx1)
        Bxs2 = [Buf("xs2_0"), Buf("xs2_1")]
        handoff([Bconst, Badax], Bxs2)
        xs2 = [U[:, 3072:4096], U[:, 4096:5120]]
        xn2 = [U[:, 5120 + i * 1024:5120 + (i + 1) * 1024] for i in range(4)]
        Bxn2 = [Buf("xn2_%d" % i) for i in range(4)]
        handoff(Bp2t[0] + Bp2t[1], Bxn2)

        def p4_A(t):
            norm_A(t, x1[:, t, :], [Bx1[t]], xn2[t % 4], Bxn2[t % 4])

        def p4_B(tb):
            norm_flush()
            ts_ = range(tb * 4, tb * 4 + 4)
            norm_B_block(tb, [xn2[t % 4] for t in ts_], [Bxn2[t % 4] for t in ts_], c_A2, c_B2, BAB2)

        cnt = 0
        for t in range(NT):
            s2 = t % 2
            ADD("sp", lambda e, t=t, s2=s2: e.dma_start(out=xs2[s2], in_=x_d[t * P:(t + 1) * P, :]), writes=[Bxs2[s2]], dma_key="xs2_%d" % s2)
            for hf in range(2):
                bank = 2 + cnt % 6
                cnt += 1
                hs = slice(hf * 512, (hf + 1) * 512)
                Wm = ring_slot(hf)
                for dc in range(KC):
                    ADD("pe", lambda e, dc=dc, bank=bank, Wm=Wm, t=t: e.matmul(PS[bank][:, :], lhsT=mergedT[:, dc, t * P:(t + 1) * P],
                                                                              rhs=Wm[:, dc, :], start=(dc == 0), stop=(dc == KC - 1)),
                        reads=[Bmg[t // 4], Bring[hf]], writes=[BPS[bank]])
                ADD("dve", lambda e, bank=bank, t=t, hs=hs: e.tensor_tensor(out=x1[:, t, hs], in0=PS[bank][:, :], in1=gtm[:, hs], op=ALU.mult),
                    reads=[BPS[bank], Bgtm], writes=[Bx1[t]])
                ADD("pool", lambda e, t=t, hs=hs, s2=s2: e.tensor_tensor(out=x1[:, t, hs], in0=x1[:, t, hs], in1=xs2[s2][:, hs], op=ALU.add),
                    reads=[Bx1[t], Bxs2[s2]], writes=[Bx1[t]])
            if t >= 5 and (t - 5) % 4 == 0:
                p4_B((t - 5) // 4)
            if t >= 1:
                p4_A(t - 1)
            if t == 3:
                ada_bcast(0, gtf, Bgtf, 2, 0)
                ada_bcast(1, gtf, Bgtf, 3, 1)
                load_gu(0)
        p4_A(NT - 1)
        if stage == 3:
            for t in range(NT):
                final_waits.append(ADD("sp", lambda e, t=t: e.dma_start(out=out_d[t * P:(t + 1) * P, :], in_=x1[:, t, :]),
                                       reads=[Bx1[t]], dma_key="out"))
            finish()
            return nc

        wdv = wd_d.rearrange("(c p) n -> p c n", p=P)
        Wdb = [Ub[:, 10240:14336].rearrange("p (c n) -> p c n", c=4), Ub[:, 14336:18432].rearrange("p (c n) -> p c n", c=4)]
        BWdb = [Buf("wdb0"), Buf("wdb1")]
        stg = [U[:, 3072:4096], U[:, 4096:5120]]
        Bstg = [Buf("stg0"), Buf("stg1")]
        sgf = [U[:, 0:512], U[:, 512:1024]]
        Bsgf = [Buf("sgf0"), Buf("sgf1")]
        actT = [Eb[:, 0:8192].rearrange("p (c s) -> p c s", c=4), Eb[:, 8192:16384].rearrange("p (c s) -> p c s", c=4)]
        Bact = [[Buf("act%d_%d" % (i, tb)) for tb in range(4)] for i in range(2)]

        stg_cnt = [0]

        def load_wd(g):
            start, n = FFN_GROUPS[g]
            for ci in range(n):
                k = stg_cnt[0] % 2
                stg_cnt[0] += 1
                ADD("sp", lambda e, k=k, c=start + ci: e.dma_start(out=stg[k], in_=wdv[:, c, :]), writes=[Bstg[k]], dma_key="stg%d" % k)
                ADD("pool", lambda e, k=k, ci=ci, g=g: e.tensor_tensor(out=Wdb[g % 2][:, ci, :], in0=stg[k], in1=gtf, op=ALU.mult),
                    reads=[Bstg[k], Bgtf], writes=[BWdb[g % 2]])

        handoff(Bxs2, Bstg)
        handoff([Bgtm], Bsgf)
        handoff(Bmg, Bact[0] + Bact[1])
        load_gu(1)

        if stage == 4:
            if dbg:
                dbg_ops.append(ADD("sp", lambda e: e.dma_start(out=dbg_d[:, 0:16384], in_=hT[:, :, :].rearrange("p k s -> p (k s)")),
                                   reads=BhT, dma_key="dbg"))
            finish()
            return nc

        gu_cnt = [0]
        dn_cnt = [0]

        def gu(g, tb):
            start, n = FFN_GROUPS[g]
            b = g % 2
            rb = (g + 1) % 2
            W = ring[:, rb * 8192:rb * 8192 + 2048 * n].rearrange("p (k n) -> p k n", k=KC)
            tsl = slice(tb * 512, (tb + 1) * 512)
            for ci in range(n):
                st = gu_cnt[0] % 2
                gu_cnt[0] += 1
                gb, ub = (0, 1) if st == 0 else (2, 3)
                for kc in range(KC):
                    ADD("pe", lambda e, kc=kc, ci=ci, gb=gb: e.matmul(PS[gb][:, :], lhsT=W[:, kc, ci * P:(ci + 1) * P], rhs=hT[:, kc, tsl],
                                                                      start=(kc == 0), stop=(kc == KC - 1)),
                        reads=[Bring[2 * rb], Bring[2 * rb + 1]] + BhT[tb * 4:(tb + 1) * 4], writes=[BPS[gb]])
                for kc in range(KC):
                    ADD("pe", lambda e, kc=kc, ci=ci, ub=ub, n=n: e.matmul(PS[ub][:, :], lhsT=W[:, kc, (n + ci) * P:(n + ci + 1) * P], rhs=hT[:, kc, tsl],
                                                                      start=(kc == 0), stop=(kc == KC - 1)),
                        reads=[Bring[2 * rb], Bring[2 * rb + 1]] + BhT[tb * 4:(tb + 1) * 4], writes=[BPS[ub]])
                ADD("act", lambda e, st=st, gb=gb: e.activation(out=sgf[st], in_=PS[gb][:, :], func=AF.Silu),
                    reads=[BPS[gb]], writes=[Bsgf[st]])
                ADD("dve", lambda e, st=st, ub=ub, ci=ci: e.tensor_tensor(out=actT[b][:, ci, tsl], in0=PS[ub][:, :], in1=sgf[st], op=ALU.mult),
                    reads=[BPS[ub], Bsgf[st]], writes=[Bact[b][tb]])

        def down(g, tb):
            start, n = FFN_GROUPS[g]
            b = g % 2
            for t in range(tb * 4, tb * 4 + 4):
                for hf in range(2):
                    bank = 4 + dn_cnt[0] % 4
                    dn_cnt[0] += 1
                    hs = slice(hf * 512, (hf + 1) * 512)
                    for ci in range(n):
                        ADD("pe", lambda e, ci=ci, bank=bank, t=t, hs=hs: e.matmul(PS[bank][:, :], lhsT=actT[b][:, ci, t * P:(t + 1) * P],
                                                                                  rhs=Wdb[b][:, ci, hs], start=(ci == 0), stop=(ci == n - 1)),
                            reads=[Bact[b][tb], BWdb[b]], writes=[BPS[bank]])
                    ADD("dve", lambda e, bank=bank, t=t, hs=hs: e.tensor_tensor(out=x1[:, t, hs], in0=PS[bank][:, :], in1=x1[:, t, hs], op=ALU.add),
                        reads=[BPS[bank], Bx1[t]], writes=[Bx1[t]])

        def final_block(tb):
            ts_ = list(range(tb * 4, tb * 4 + 4))
            for t in ts_:
                ADD("act", lambda e, t=t: e.activation(out=stg[t % 2], in_=x1[:, t, :], func=AF.Square, accum_out=c_ss[:, t:t + 1]),
                    reads=[Bx1[t], Bstg[t % 2]], writes=[Bstg[t % 2], Bss[t]])
            for t in ts_:
                ADD("dve", lambda e, t=t: e.tensor_scalar(out=c_tmp[:, t:t + 1], in0=c_ss[:, t:t + 1], scalar1=float(1.0 / D),
                                                          scalar2=float(EPS), op0=ALU.mult, op1=ALU.add), reads=[Bss[t]], writes=[Bss[t]])
            for t in ts_:
                ADD("pool", lambda e, t=t: e.tensor_tensor(out=c_rs[:, t:t + 1], in0=c_tmp[:, t:t + 1], in1=c_mhalf, op=ALU.pow),
                    reads=[Bss[t], Bcols], writes=[Bss[t]])
            for t in ts_:
                ADD("dve", lambda e, t=t: e.scalar_tensor_tensor(out=x1[:, t, :], in0=x1[:, t, :], scalar=c_rs[:, t:t + 1], in1=gfin,
                                                                 op0=ALU.mult, op1=ALU.mult), reads=[Bx1[t], Bss[t], Bgfin], writes=[Bx1[t]])
                final_waits.append(ADD("sp", lambda e, t=t: e.dma_start(out=out_d[t * P:(t + 1) * P, :], in_=x1[:, t, :]),
                                       reads=[Bx1[t]], dma_key="out"))

        def down_merged(gs, tb):
            chunks = [(g_ % 2, ci) for g_ in gs for ci in range(FFN_GROUPS[g_][1])]
            for t in range(tb * 4, tb * 4 + 4):
                for hf in range(2):
                    bank = 4 + dn_cnt[0] % 4
                    dn_cnt[0] += 1
                    hs = slice(hf * 512, (hf + 1) * 512)
                    for k_, (b_, ci) in enumerate(chunks):
                        ADD("pe", lambda e, ci=ci, b_=b_, bank=bank, t=t, hs=hs, k_=k_: e.matmul(
                            PS[bank][:, :], lhsT=actT[b_][:, ci, t * P:(t + 1) * P], rhs=Wdb[b_][:, ci, hs],
                            start=(k_ == 0), stop=(k_ == len(chunks) - 1)),
                            reads=[Bact[0][tb], Bact[1][tb], BWdb[0], BWdb[1]], writes=[BPS[bank]])
                    ADD("dve", lambda e, bank=bank, t=t, hs=hs: e.tensor_tensor(out=x1[:, t, hs], in0=PS[bank][:, :], in1=x1[:, t, hs], op=ALU.add),
                        reads=[BPS[bank], Bx1[t]], writes=[Bx1[t]])

        NG = len(FFN_GROUPS)
        for g in range(NG):
            if g >= 1:
                if g + 1 < NG:
                    load_gu(g + 1)
            for tb in range(4):
                gu(g, tb)
                if g == 0 and tb == 0:
                    p4_B(3)
                    handoff(Bp2t[0] + Bp2t[1] + Bxn2, BWdb)
                    load_wd(0)
                if 1 <= g <= NG - 2:
                    down(g - 1, tb)
            if g + 1 < NG:
                load_wd(g + 1)
        for tb in range(4):
            down_merged([NG - 2, NG - 1], tb)
            final_block(tb)
        finish()
    return nc


def _consts():
    h = np.arange(H, dtype=np.float64)
    log_gamma = np.log(1.0 - 2.0 ** (-5.0 - h))
    idx = np.arange(P, dtype=np.float64)
    dscale = float(P) ** -0.5
    mask = np.zeros((P, H, P), dtype=np.float64)
    for hh in range(H):
        col = dscale * np.exp(-log_gamma[hh] * (idx + 1.0))
        mm = np.where(idx[None, :] >= idx[:, None], 1.0, 0.0)
        mask[:, hh, :] = mm * col[:, None]
    zeta = dscale * np.exp(log_gamma[None, :] * (P - 1 - idx)[:, None])
    xi = np.exp(log_gamma[None, :] * (idx + 1.0)[:, None])
    epsp = EPS / (xi * xi)
    inv_freq = 1.0 / (10000.0 ** (np.arange(0, P, 2, dtype=np.float32) / np.float32(P)))
    invf = np.broadcast_to(inv_freq.astype(np.float32)[None, :], (P, 64))
    return (np.ascontiguousarray(mask.reshape(P, H * P).astype(np.float32)), np.ascontiguousarray(zeta.astype(np.float32)),
            np.ascontiguousarray(epsp.astype(np.float32)), np.ascontiguousarray(invf.astype(np.float32)),
            np.eye(P, dtype=np.float32))


def make_in_maps(x, c, positions, ada_w, ada_b, norm_mix_g, w_in, conv_w, ret_w_out, conv_w_out, mix_w_out,
                 norm_ffn_g, ffn_w_gate, ffn_w_up, ffn_w_down, final_norm_g, n_cores=8):
    f = lambda a: np.ascontiguousarray(np.asarray(a, dtype=np.float32))
    mask, zeta, epsp, invf, ident = _consts()
    col8 = lambda v: np.ascontiguousarray(np.asarray(v, dtype=np.float32).reshape(-1, P).T)
    adaw = f(ada_w[0]); win = f(w_in[0])
    adaw_l = adaw.reshape(8, P, 12, 512).transpose(2, 1, 0, 3).reshape(12 * P, 4096)
    w1_l = win[:, 0:4096].reshape(8, P, 4, 8, P).transpose(3, 1, 0, 2, 4).reshape(8 * P, 4096)
    wc_l = win[:, 4096:7168].reshape(8, P, 3, 8, P).transpose(3, 1, 0, 2, 4).reshape(8 * P, 3072)
    st4 = np.stack([f(ret_w_out[0]), f(conv_w_out[0]), win[:, 7168:8192], win[:, 8192:9216]], axis=1)
    wp2_l = st4.reshape(8, P, 4, 8, P).transpose(3, 1, 0, 2, 4).reshape(8 * P, 4096)
    mix_l = f(mix_w_out[0]).reshape(8, P, 2, 512).transpose(2, 1, 0, 3).reshape(2 * P, 4096)
    wg = f(ffn_w_gate[0]).reshape(8, P, HID); wu = f(ffn_w_up[0]).reshape(8, P, HID)
    shared = {
        "adaw_l": np.ascontiguousarray(adaw_l), "adab_col": col8(ada_b[0]), "adab_row": f(ada_b[0]).reshape(1, -1),
        "gmix_col": col8(norm_mix_g[0]), "gffn_col": col8(norm_ffn_g[0]), "gfin_row": f(final_norm_g).reshape(1, -1),
        "w1_l": np.ascontiguousarray(w1_l), "wc_l": np.ascontiguousarray(wc_l), "wp2_l": np.ascontiguousarray(wp2_l),
        "mix_l": np.ascontiguousarray(mix_l),
        "convw_col": np.ascontiguousarray(f(conv_w[0]).reshape(3, 8, P).transpose(2, 1, 0).reshape(P, 24)),
        "ffn_w_down": f(ffn_w_down[0]),
        "invf": invf, "maskT": mask, "zeta": zeta, "epsp": epsp, "ident_in": ident,
    }
    for g, (st_, n) in enumerate(FFN_GROUPS):
        gg = wg[:, :, st_ * P:(st_ + n) * P]; uu = wu[:, :, st_ * P:(st_ + n) * P]
        shared["wgu_l%d" % g] = np.ascontiguousarray(np.stack([gg, uu], axis=2).transpose(1, 0, 2, 3).reshape(P, 2048 * n))
    maps = []
    xs = np.asarray(x, dtype=np.float32)
    cs = np.asarray(c, dtype=np.float32)
    ps = np.asarray(positions, dtype=np.int32)
    for b in range(n_cores):
        m = dict(shared)
        m["x"] = np.ascontiguousarray(xs[b])
        m["ccol"] = col8(cs[b])
        m["pos"] = np.ascontiguousarray(ps[b].reshape(NT, P).T)
        maps.append(m)
    return maps


_NC_CACHE = {}


def kernel(**inputs):
    if "nc" not in _NC_CACHE:
        _NC_CACHE["nc"] = build_nc()
    nc = _NC_CACHE["nc"]
    in_maps = make_in_maps(**inputs)
    res = run_bass_kernel_spmd(nc, in_maps, core_ids=list(range(8)))
    out = np.stack([np.asarray(r["out"], dtype=np.float32) for r in res.results], axis=0)
    return out
```

```python
import numpy as np
from contextlib import ExitStack
import concourse.bass as bass
import concourse.mybir as mybir
from concourse.bass_utils import run_bass_kernel_spmd

F32 = mybir.dt.float32
BF16 = mybir.dt.bfloat16
I32 = mybir.dt.int32
AF = mybir.ActivationFunctionType
ALU = mybir.AluOpType

P = 128
SEQ = 2048
NT = 16
D = 1024
KC = 8
H = 8
HID = 2816
NHC = 22
EPS = 1e-6
TWO_PI = 2.0 * np.pi

ENGS = ("pe", "act", "dve", "pool", "sp")
import os
CUT = int(os.environ.get("CUT", "1000000000"))
NOSELF = os.environ.get("NOSELF", "")


class Buf:
    __slots__ = ("name", "writers", "readers", "excl", "last")

    def __init__(self, name, excl=False):
        self.name = name
        self.writers = []
        self.readers = []
        self.excl = excl
        self.last = {}


class Op:
    __slots__ = ("eng", "fn", "deps", "idx", "ticket", "dma_key", "dma_val", "needs_inc", "is_dma")


class Sched:
    def __init__(self):
        self.ops = []
        self.q = {e: [] for e in ENGS}
        self.dma_counts = {}

    def add(self, eng, fn, reads=(), writes=(), dma_key=None, extra_deps=()):
        op = Op()
        op.eng = eng
        op.fn = fn
        op.idx = len(self.ops)
        op.is_dma = dma_key is not None
        op.dma_key = dma_key
        op.needs_inc = False
        op.ticket = None
        op.dma_val = None
        deps = []
        for b in reads:
            deps.extend(b.writers)
        for b in writes:
            if b.readers:
                deps.extend(b.readers)
                deps.extend(b.writers)
        deps.extend(extra_deps)
        for b in list(reads) + list(writes):
            if b.excl:
                for e2, o2 in b.last.items():
                    if e2 != eng:
                        deps.append(o2)
        best_e = {}
        best_d = {}
        for d in deps:
            if d is op:
                continue
            if d.is_dma:
                if d.dma_key not in best_d or best_d[d.dma_key].dma_val < d.dma_val:
                    best_d[d.dma_key] = d
            else:
                if d.eng not in best_e or best_e[d.eng].idx < d.idx:
                    best_e[d.eng] = d
        op.deps = list(best_e.values()) + list(best_d.values())
        if op.is_dma:
            self.dma_counts[dma_key] = self.dma_counts.get(dma_key, 0) + 16
            op.dma_val = self.dma_counts[dma_key]
        for b in list(reads) + list(writes):
            if b.excl:
                b.last[eng] = op
        for b in reads:
            b.readers.append(op)
        for b in writes:
            if b.readers:
                b.readers = []
                b.writers = [op]
            else:
                b.writers.append(op)
        self.ops.append(op)
        self.q[eng].append(op)
        return op

    def emit(self, nc, final_waits=()):
        for op in self.ops:
            for d in op.deps:
                if not d.is_dma:
                    d.needs_inc = True
        for e in ENGS:
            t = 0
            for op in self.q[e]:
                if not op.is_dma and op.needs_inc:
                    t += 1
                    op.ticket = t
        with ExitStack() as es:
            esem = {e: es.enter_context(nc.semaphore("s_" + e)) for e in ENGS}
            dsem = {k: es.enter_context(nc.semaphore("d_%s" % (k,))) for k in self.dma_counts}
            block = es.enter_context(nc.Block())

            def make(e):
                def body(engine):
                    seen_e = {x: 0 for x in ENGS}
                    seen_d = {k: 0 for k in self.dma_counts}
                    for op in self.q[e]:
                        if op.idx >= CUT:
                            break
                        for d in op.deps:
                            if d.is_dma:
                                if seen_d[d.dma_key] < d.dma_val:
                                    engine.wait_ge(dsem[d.dma_key], d.dma_val)
                                    seen_d[d.dma_key] = d.dma_val
                            else:
                                if d.eng == e and NOSELF and e in NOSELF.split(","):
                                    continue
                                if seen_e[d.eng] < d.ticket:
                                    engine.wait_ge(esem[d.eng], d.ticket)
                                    seen_e[d.eng] = d.ticket
                        ins = op.fn(engine)
                        if op.is_dma:
                            ins.then_inc(dsem[op.dma_key], 16)
                        elif op.needs_inc:
                            ins.then_inc(esem[e], 1)
                    if e == "sp":
                        fin = {}
                        for d in final_waits:
                            if d.idx >= CUT:
                                continue
                            fin[d.dma_key] = max(fin.get(d.dma_key, 0), d.dma_val)
                        if CUT < 1000000000:
                            for op2 in self.ops:
                                if op2.is_dma and op2.idx < CUT:
                                    fin[op2.dma_key] = max(fin.get(op2.dma_key, 0), op2.dma_val)
                        for k, v in fin.items():
                            if seen_d[k] < v:
                                engine.wait_ge(dsem[k], v)
                                seen_d[k] = v
                return body

            block.tensor(make("pe"))
            block.scalar(make("act"))
            block.vector(make("dve"))
            block.gpsimd(make("pool"))
            block.sync(make("sp"))


def handoff(old_bufs, new_bufs):
    users = []
    for b in old_bufs:
        users.extend(b.readers)
        users.extend(b.writers)
    for b in new_bufs:
        b.readers = list(b.readers) + users


FFN_GROUPS = [(0, 4), (4, 4), (8, 4), (12, 4), (16, 3), (19, 3)]


def build_nc(stage=99, dbg=False):
    nc = bass.Bass("TRN2", target_bir_lowering=False)

    def din(name, shape, dt=F32):
        return nc.dram_tensor(name, shape, dt, kind="ExternalInput").ap()

    x_d = din("x", [SEQ, D])
    ccol_d = din("ccol", [P, KC])
    pos_d = din("pos", [P, NT], I32)
    adaw_d = din("adaw_l", [12 * P, 4096])
    adabc_d = din("adab_col", [P, 48])
    adabr_d = din("adab_row", [1, 6 * D])
    gmix_d = din("gmix_col", [P, KC])
    gffn_d = din("gffn_col", [P, KC])
    gfin_d = din("gfin_row", [1, D])
    w1_d = din("w1_l", [8 * P, 4096])
    wc_d = din("wc_l", [8 * P, 3072])
    wp2_d = din("wp2_l", [8 * P, 4096])
    mixl_d = din("mix_l", [2 * P, 4096])
    convw_d = din("convw_col", [P, 8 * 3])
    wgu_d = [din("wgu_l%d" % g, [P, 2048 * n]) for g, (st_, n) in enumerate(FFN_GROUPS)]
    wd_d = din("ffn_w_down", [HID, D])
    invf_d = din("invf", [P, 64])
    mask_d = din("maskT", [P, H * P])
    zeta_d = din("zeta", [P, H])
    epsp_d = din("epsp", [P, H])
    ident_d = din("ident_in", [P, P])
    out_d = nc.dram_tensor("out", [SEQ, D], F32, kind="ExternalOutput").ap()
    if dbg:
        dbg_d = nc.dram_tensor("dbg", [P, 32768], BF16, kind="ExternalOutput").ap()

    S = Sched()
    global _LAST_SCHED
    _LAST_SCHED = S
    decay = [float(np.exp(np.float64(np.log(1.0 - 2.0 ** (-5.0 - h))) * 128.0)) for h in range(H)]

    with ExitStack() as es:
        def sb(name, shape, dt=F32):
            return es.enter_context(nc.sbuf_tensor(name, shape, dt))

        hT = sb("hT", [P, KC, SEQ], BF16)
        BC = sb("BC", [P, 16384], F32)
        BCb = BC[:, :].bitcast(BF16)
        ygT = BCb[:, 0:16384].rearrange("p (h s) -> p h s", h=KC)
        bcT = BCb[:, 16384:32768].rearrange("p (h s) -> p h s", h=KC)
        x1 = BC[:, :].rearrange("p (t d) -> p t d", t=NT)
        E = sb("E", [P, 8192], F32)
        Eb = E[:, :].bitcast(BF16)
        mergedT = Eb.rearrange("p (h s) -> p h s", h=KC)
        ring = sb("ring", [P, 16384], BF16)
        U = sb("U", [P, 9216], F32)
        Ub = U[:, :].bitcast(BF16)

        ident = sb("ident", [P, P])
        identb = sb("identb", [P, P], BF16)
        gtm = U[:, 0:1024]
        gtf = U[:, 1024:2048]
        gfin = U[:, 2048:3072]
        cols = sb("cols", [P, 256])
        csrep = sb("csrep", [P, KC, P], BF16)
        Rst = sb("Rst", [P, P])
        Rb = sb("Rb", [P, 4, P], BF16)
        posi = sb("posi", [P, NT], I32)
        tmp8 = sb("tmp8", [P, 8])
        sqj = sb("sqj", [P, D], BF16)
        Bsqj = Buf("sqj")
        sqj2 = sb("sqj2", [P, D], BF16)
        Bsqj2 = Buf("sqj2")
        dbgt = sb("dbgt", [P, 128])

        c_ccol = cols[:, 0:8]
        c_cs = cols[:, 8:16]
        c_gmix = cols[:, 16:24]
        c_gffn = cols[:, 24:32]
        c_adab = cols[:, 32:80]
        c_convw = cols[:, 80:104]
        c_zeta = cols[:, 104:112]
        c_epsp = cols[:, 112:120]
        c_A1 = cols[:, 120:128]
        c_B1 = cols[:, 128:136]
        c_A2 = cols[:, 136:144]
        c_B2 = cols[:, 144:152]
        c_mhalf = cols[:, 152:153]
        c_posf = cols[:, 160:176]
        c_ss = cols[:, 176:192]
        c_rs = cols[:, 192:208]
        c_tmp = cols[:, 208:224]
        c_ln = cols[:, 224:256]
        csbf = sb("csbf", [P, KC], BF16)
        lnst = sb("lnst", [P, 8, 8])

        cosT = U[:, 0:1024].rearrange("p (t j) -> p t j", t=NT)
        sinT = U[:, 1024:2048].rearrange("p (t j) -> p t j", t=NT)
        nsinT = U[:, 2048:3072].rearrange("p (t j) -> p t j", t=NT)
        maskT = U[:, 3072:4096].rearrange("p (h c) -> p h c", h=H)
        p1 = BC[:, 0:4096]
        cvt = BC[:, 4096:8192]
        PS = [es.enter_context(nc.psum_tensor("ps%d" % i, [P, 512], F32)) for i in range(8)]
        BPS = [Buf("ps%d" % i, excl=True) for i in range(8)]

        Bconst = Buf("const")
        BhT = [Buf("hT%d" % t) for t in range(NT)]
        Byg = [Buf("yg%d" % i) for i in range(4)]
        Bbc = [Buf("bc%d" % i) for i in range(4)]
        Bx1 = [Buf("x1_%d" % t) for t in range(NT)]
        Bmg = [Buf("mg%d" % i) for i in range(4)]
        Bring = [Buf("ring%d" % i) for i in range(4)]
        Bcols = Buf("cols_setup")
        BAB1 = Buf("AB1")
        BAB2 = Buf("AB2")
        Bgtm = Buf("gtm")
        Bgtf = Buf("gtf")
        Bgfin = Buf("gfin")
        Brope = Buf("rope")
        Bcsrep = Buf("csrep")

        def ring_slot(s):
            return ring[:, s * 4096:(s + 1) * 4096].rearrange("p (k n) -> p k n", k=KC)

        def ring_big(b):
            return ring[:, b * 8192:(b + 1) * 8192].rearrange("p (k n) -> p k n", k=KC)

        def wview(w_ap):
            return w_ap.rearrange("(k p) n -> p k n", p=P)

        ADD = S.add

        cdma_n = [0]

        def cdma(out_ap, in_ap):
            q = "sp" if cdma_n[0] % 2 == 0 else "act"
            cdma_n[0] += 1
            ADD(q, lambda e: e.dma_start(out=out_ap, in_=in_ap), writes=[Bconst], dma_key="const")

        cdma(c_ccol, ccol_d[:, :])
        cdma(posi[:, :], pos_d[:, :])
        cdma(c_adab, adabc_d[:, :])
        cdma(c_gmix, gmix_d[:, :])
        cdma(c_gffn, gffn_d[:, :])
        cdma(c_convw, convw_d[:, :])
        cdma(c_zeta, zeta_d[:, :])
        cdma(c_epsp, epsp_d[:, :])
        cdma(ident[:, :], ident_d[:, :])
        cdma(p1[:, 0:64], invf_d[:, :])
        cdma(maskT, mask_d[:, :].rearrange("p (h c) -> p h c", h=H))

        wload_extra = [()]

        def wload(dst_flat, src_flat, bufs, key):
            L = dst_flat.shape[1]
            nch = (L + 2047) // 2048
            assert L % nch == 0
            d3 = dst_flat.rearrange("p (c n) -> p c n", c=nch)
            s3 = src_flat.rearrange("p (c n) -> p c n", c=nch)
            return ADD("pool", lambda e: e.dma_start(out=d3, in_=s3), writes=bufs, dma_key=key, extra_deps=wload_extra[0])

        def ring_flat(s, L=4096):
            return ring[:, s * 4096:s * 4096 + L]

        def load_ada(blk, slot):
            wload(ring_flat(slot), adaw_d[blk * P:(blk + 1) * P, :], [Bring[slot]], "ring%d" % slot)

        ada_first = []
        wload_extra[0] = list(Bconst.writers)
        for blk in range(4):
            load_ada(blk, blk)
            ada_first.append(S.ops[-1])
        wload_extra[0] = ()

        ADD("dve", lambda e: e.tensor_copy(out=identb[:, :], in_=ident[:, :]), reads=[Bconst], writes=[Bcols])
        ADD("dve", lambda e: e.memset(c_mhalf, -0.5), writes=[Bcols])
        ADD("act", lambda e: e.activation(out=c_cs, in_=c_ccol, func=AF.Silu), reads=[Bconst], writes=[Bcols])
        Bcsbf = Buf("csbf")
        ADD("dve", lambda e: e.tensor_copy(out=csbf[:, :], in_=c_cs), reads=[Bcols], writes=[Bcsbf])
        ADD("dve", lambda e: e.tensor_copy(out=csrep[:, :, :], in_=c_cs.unsqueeze(2).broadcast_to([P, KC, P])),
            reads=[Bcols], writes=[Bcsrep])
        ADD("dve", lambda e: e.tensor_copy(out=c_posf, in_=posi[:, :]), reads=[Bconst], writes=[Bcols])

        ang = p1[:, 64:1088].rearrange("p (t j) -> p t j", t=NT)
        uu = p1[:, 1088:2112].rearrange("p (t j) -> p t j", t=NT)
        ki = cvt[:, 0:1024].bitcast(I32).rearrange("p (t j) -> p t j", t=NT)
        ff = cvt[:, 1024:2048].rearrange("p (t j) -> p t j", t=NT)
        invf = p1[:, 0:64]
        ADD("dve", lambda e: e.memset(Rb[:, :, :], 0.0), writes=[Bcols])

        def ada_cols(blk, ps_cols, slot):
            for j in range(4):
                for kc in range(KC):
                    ADD("pe", lambda e, j=j, kc=kc: e.matmul(PS[4][:, ps_cols + j:ps_cols + j + 1],
                                                             lhsT=ring_slot(slot)[:, kc, j * P:(j + 1) * P],
                                                             rhs=csbf[:, kc:kc + 1], start=(kc == 0), stop=(kc == KC - 1)),
                        reads=[Bring[slot], Bcsbf], writes=[BPS[4]])

        def ada_bcast(blk_half, dst, dstbuf, slot, bank):
            for kc in range(KC):
                ADD("pe", lambda e, kc=kc: e.matmul(PS[bank][:, :], lhsT=csrep[:, kc, :], rhs=ring_slot(slot)[:, kc, :],
                                                    start=(kc == 0), stop=(kc == KC - 1)),
                    reads=[Bring[slot], Bcsrep], writes=[BPS[bank]])
            sl = slice(blk_half * 512, (blk_half + 1) * 512)
            ADD("dve", lambda e: e.tensor_tensor(out=dst[:, sl], in0=PS[bank][:, :], in1=dst[:, sl], op=ALU.add),
                reads=[BPS[bank], dstbuf], writes=[dstbuf])

        for blk in range(4):
            ada_cols(blk, blk * 4, blk)
        ADD("dve", lambda e: e.tensor_tensor(out=c_B1, in0=PS[4][:, 0:8], in1=c_adab[:, 0:8], op=ALU.add),
            reads=[BPS[4], Bconst], writes=[BAB1])
        ADD("dve", lambda e: e.tensor_tensor(out=tmp8[:, :], in0=PS[4][:, 8:16], in1=c_adab[:, 8:16], op=ALU.add),
            reads=[BPS[4], Bconst], writes=[BAB1])
        ADD("dve", lambda e: e.scalar_tensor_tensor(out=c_A1, in0=tmp8[:, :], scalar=1.0, in1=c_gmix,
                                                    op0=ALU.add, op1=ALU.mult), reads=[BAB1, Bconst], writes=[BAB1])

        ADD("dve", lambda e: e.tensor_tensor(out=ang, in0=c_posf.unsqueeze(2).broadcast_to([P, NT, 64]),
                                              in1=invf.unsqueeze(1).broadcast_to([P, NT, 64]), op=ALU.mult),
            reads=[Bcols, Bconst], writes=[Brope])
        for (dst, shift) in ((sinT, 0.5), (cosT, 0.75)):
            ADD("dve", lambda e, shift=shift: e.tensor_scalar(out=uu, in0=ang, scalar1=float(1.0 / TWO_PI), scalar2=shift,
                                                               op0=ALU.mult, op1=ALU.add), reads=[Brope], writes=[Brope])
            ADD("dve", lambda e: e.tensor_copy(out=ki, in_=uu), reads=[Brope], writes=[Brope])
            ADD("dve", lambda e: e.tensor_tensor(out=ff, in0=uu, in1=ki, op=ALU.subtract), reads=[Brope], writes=[Brope])
            ADD("dve", lambda e: e.tensor_scalar(out=uu, in0=ff, scalar1=0.0, scalar2=None, op0=ALU.is_lt),
                reads=[Brope], writes=[Brope])
            ADD("dve", lambda e: e.tensor_tensor(out=ff, in0=ff, in1=uu, op=ALU.add), reads=[Brope], writes=[Brope])
            ADD("dve", lambda e: e.tensor_scalar(out=ff, in0=ff, scalar1=1.0, scalar2=0.0, op0=ALU.min, op1=ALU.max),
                reads=[Brope], writes=[Brope])
            ADD("act", lambda e, dst=dst: e.activation(out=dst, in_=ff, func=AF.Sin, bias=float(-np.pi), scale=float(TWO_PI)),
                reads=[Brope, Bcols], writes=[Brope])
        ADD("dve", lambda e: e.tensor_scalar(out=nsinT, in0=sinT, scalar1=-1.0, scalar2=None, op0=ALU.mult),
            reads=[Brope], writes=[Brope])

        tp_par = [0]

        def norm_A(t, src_ap, src_bufs, xn_ap, xn_buf, inplace=False):
            if inplace:
                ADD("dve", lambda e: e.scalar_tensor_tensor(out=sqj[:, :], in0=src_ap, scalar=1.0, in1=src_ap, op0=ALU.mult,
                                                            op1=ALU.mult, accum_out=c_ss[:, t:t + 1]),
                    reads=src_bufs + [Bsqj], writes=[Bsqj, Bss[t]])
            else:
                ADD("dve", lambda e: e.scalar_tensor_tensor(out=xn_ap, in0=src_ap, scalar=1.0, in1=src_ap, op0=ALU.mult,
                                                            op1=ALU.mult, accum_out=c_ss[:, t:t + 1]),
                    reads=src_bufs, writes=[xn_buf, Bss[t]])
            ADD("dve", lambda e: e.tensor_scalar(out=c_tmp[:, t:t + 1], in0=c_ss[:, t:t + 1], scalar1=float(1.0 / D),
                                                 scalar2=float(EPS), op0=ALU.mult, op1=ALU.add),
                reads=[Bss[t]], writes=[Bss[t]])
            ADD("pool", lambda e: e.tensor_tensor(out=c_rs[:, t:t + 1], in0=c_tmp[:, t:t + 1], in1=c_mhalf, op=ALU.pow),
                reads=[Bss[t], Bcols], writes=[Bss[t]])
            def stage2():
                ADD("act", lambda e: e.activation(out=xn_ap, in_=src_ap, func=AF.Identity, scale=c_rs[:, t:t + 1]),
                    reads=src_bufs + [Bss[t]], writes=[xn_buf])
            norm_pend.append(stage2)
            while len(norm_pend) > 1:
                norm_pend.pop(0)()

        norm_pend = []

        def norm_flush():
            while norm_pend:
                norm_pend.pop(0)()

        def norm_B_block(tb, xn_aps, xn_bufs, Acol, Bcol, ABbuf):
            for half in range(2):
                for kq in range(4):
                    kc = half * 4 + kq
                    bank = kc
                    for i in range(4):
                        ADD("pe", lambda e, kc=kc, bank=bank, i=i: e.transpose(out=PS[bank][:, i * P:(i + 1) * P],
                                                                               in_=xn_aps[i][:, kc * P:(kc + 1) * P], identity=ident[:, :]),
                            reads=[xn_bufs[i], Bconst], writes=[BPS[bank]])
                for kq in range(4):
                    kc = half * 4 + kq
                    bank = kc
                    dst = hT[:, kc, tb * 512:(tb + 1) * 512]
                    if half == 0:
                        ADD("act", lambda e, bank=bank, dst=dst, kc=kc: e.activation(out=dst, in_=PS[bank][:, :], func=AF.Identity,
                                                                                     bias=Bcol[:, kc:kc + 1], scale=Acol[:, kc:kc + 1]),
                            reads=[BPS[bank], ABbuf], writes=BhT[tb * 4:(tb + 1) * 4])
                    else:
                        ADD("dve", lambda e, bank=bank, dst=dst, kc=kc: e.tensor_scalar(out=dst, in0=PS[bank][:, :], scalar1=Acol[:, kc:kc + 1],
                                                                                        scalar2=Bcol[:, kc:kc + 1], op0=ALU.mult, op1=ALU.add),
                            reads=[BPS[bank], ABbuf], writes=BhT[tb * 4:(tb + 1) * 4])

        def norm_schedule(A_fn, B_fn, R=6):
            done_B = -1
            nextA = 0
            for tb in range(4):
                while nextA < NT and (nextA - R) // 4 <= done_B and nextA < 4 * tb + R:
                    A_fn(nextA)
                    nextA += 1
                norm_flush()
                B_fn(tb)
                done_B = tb
            assert nextA == NT

        Bss = [Buf("ss%d" % t) for t in range(NT)]

        Bxs = []
        Bxn = [Buf("xn%d" % i) for i in range(8)]
        xn_slot = [E[:, i * 1024:(i + 1) * 1024] for i in range(8)]

        def p0_A(t):
            s_ = t % 8
            ADD("sp", lambda e, s_=s_, t=t: e.dma_start(out=xn_slot[s_], in_=x_d[t * P:(t + 1) * P, :]),
                writes=[Bxn[s_]], dma_key="xn%d" % s_, extra_deps=(ada_first if t >= 2 else ()))
            norm_A(t, xn_slot[s_], [Bxn[s_]], xn_slot[s_], Bxn[s_], inplace=True)

        def p0_B(tb):
            ts_ = range(tb * 4, tb * 4 + 4)
            norm_B_block(tb, [xn_slot[t % 8] for t in ts_], [Bxn[t % 8] for t in ts_], c_A1, c_B1, BAB1)

        norm_schedule(p0_A, p0_B, R=8)

        final_waits = []
        dbg_ops = []

        def finish():
            S.emit(nc, final_waits=final_waits + dbg_ops)

        if stage == 0:
            if dbg:
                dbg_ops.append(ADD("sp", lambda e: e.dma_start(out=dbg_d[:, 0:16384], in_=hT[:, :, :].rearrange("p k s -> p (k s)")),
                                   reads=BhT, dma_key="dbg"))
            finish()
            return nc

        handoff([Brope] + Bxs + Bxn, [])
        def load_qkvg(h):
            s_ = h % 2
            wload(ring_flat(s_), w1_d[h * P:(h + 1) * P, :], [Bring[s_]], "ring%d" % s_)

        def load_conv(g):
            cs_ = 2 + g % 2
            wload(ring_flat(cs_, 3072), wc_d[g * P:(g + 1) * P, :], [Bring[cs_]], "ring%d" % cs_)

        load_qkvg(0)
        load_conv(0)
        load_qkvg(1)

        def mk(off, width, depth, dt=F32):
            aps = []
            for i in range(depth):
                a = E[:, off + i * width: off + (i + 1) * width]
                if dt == BF16:
                    a = a.bitcast(BF16)
                aps.append(a)
            return aps, off + depth * width
        o = 0
        tA, o = mk(o, 256, 2)
        tB, o = mk(o, 256, 2)
        tQK, o = mk(o, 128, 4, BF16)
        tvb, o = mk(o, 64, 4, BF16)
        tvz, o = mk(o, 64, 4, BF16)
        tsg, o = mk(o, 128, 8)
        tqkT, o = mk(o, 128, 4, BF16)
        tsm, o = mk(o, 64, 4, BF16)
        ty, o = mk(o, 128, 4)
        tyg, o = mk(o, 64, 4, BF16)
        assert o <= 5120
        cv_c = E[:, 5120:5632]
        cv_u = [E[:, 5632:6148], E[:, 6148:6664]]
        cv_t = E[:, 6664:7176]
        cv_b = E[:, 7176:7688]
        BtA = [Buf("tA%d" % i) for i in range(2)]
        BtB = [Buf("tB%d" % i) for i in range(2)]
        BtQK = [Buf("tQK%d" % i) for i in range(4)]
        Btvb = [Buf("tvb%d" % i) for i in range(4)]
        Btvz = [Buf("tvz%d" % i) for i in range(4)]
        Btsg = [Buf("tsg%d" % i) for i in range(8)]
        BtqkT = [Buf("tqkT%d" % i) for i in range(4)]
        Btsm = [Buf("tsm%d" % i) for i in range(4)]
        Bty = [Buf("ty%d" % i) for i in range(4)]
        Btyg = [Buf("tyg%d" % i) for i in range(4)]
        Bln = [Buf("ln%d" % i) for i in range(8)]
        Bcvc, Bcvt, Bcvb = Buf("cvc"), Buf("cvt"), Buf("cvb")
        Bcvu = [Buf("cvu0"), Buf("cvu1")]
        BRst = Buf("Rst")
        BRb = [Buf("Rb%d" % i) for i in range(4)]
        e_p1 = BtA + BtB + BtQK + Btvb + Btvz + Btsg + BtqkT + Btsm + Bty + Btyg + [Bcvc, Bcvt, Bcvb] + Bcvu
        handoff(Bxs + Bxn, e_p1)
        handoff([Brope], Byg + Bbc)
        qkT_ps = [PS[2][:, 0:128].bitcast(BF16), PS[2][:, 128:256].bitcast(BF16)]
        ygps = [PS[2][:, 256 + i * 64:256 + (i + 1) * 64].bitcast(BF16) for i in range(4)]
        sc_ps = [PS[3][:, 0:128], PS[3][:, 256:384]]
        kv_ps = [PS[3][:, 128:256], PS[3][:, 384:512]]
        out_ps = [PS[4][:, 0:128], PS[4][:, 128:256]]
        Bqkps = [BPS[2], BPS[2]]
        Bygps = [BPS[2]] * 4
        Bscps = [BPS[3], BPS[3]]
        Bkvps = [BPS[3], BPS[3]]
        Boutps = [BPS[4], BPS[4]]
        Badaps = BPS[4]

        def S1(G, mid=None):
            h, t = divmod(G, NT)
            s = h % 2
            W = ring_slot(s)
            for half in range(2):
                for kc in range(KC):
                    ADD("pe", lambda e, kc=kc, half=half: e.matmul(PS[half][:, 0:256], lhsT=hT[:, kc, t * P:(t + 1) * P],
                                                                   rhs=W[:, kc, half * 256:(half + 1) * 256],
                                                                   start=(kc == 0), stop=(kc == KC - 1)),
                        reads=[BhT[t], Bring[s]], writes=[BPS[half]])
                if half == 0 and mid is not None:
                    mid()

        def S2(G):
            h, t = divmod(G, NT)
            pqa = PS[0]
            pqb = PS[1]
            T4 = pqa[:, 0:256].rearrange("p (a b j) -> p a b j", a=2, b=2)
            A4 = tA[G % 2].rearrange("p (a b j) -> p a b j", a=2, b=2)
            B4 = tB[G % 2].rearrange("p (a b j) -> p a b j", a=2, b=2)
            CC = cosT[:, t, :].unsqueeze(1).unsqueeze(1).broadcast_to([P, 2, 2, 64])
            SN = sinT[:, t, :].unsqueeze(1).broadcast_to([P, 2, 64])
            NS = nsinT[:, t, :].unsqueeze(1).broadcast_to([P, 2, 64])
            ADD("dve", lambda e: e.tensor_tensor(out=A4, in0=T4, in1=CC, op=ALU.mult),
                reads=[BPS[0], Brope], writes=[BtA[G % 2]])
            ADD("dve", lambda e: e.tensor_tensor(out=B4[:, :, 0, :], in0=T4[:, :, 1, :], in1=NS, op=ALU.mult),
                reads=[BPS[0], Brope], writes=[BtB[G % 2]])
            ADD("dve", lambda e: e.tensor_tensor(out=B4[:, :, 1, :], in0=T4[:, :, 0, :], in1=SN, op=ALU.mult),
                reads=[BPS[0], Brope], writes=[BtB[G % 2]])
            ADD("pool", lambda e: e.tensor_tensor(out=tQK[G % 4], in0=tA[G % 2], in1=tB[G % 2], op=ALU.add),
                reads=[BtA[G % 2], BtB[G % 2]], writes=[BtQK[G % 4]])
            ADD("act", lambda e: e.activation(out=tvb[G % 4], in_=pqb[:, 0:128], func=AF.Identity),
                reads=[BPS[1]], writes=[Btvb[G % 4]])
            ADD("act", lambda e: e.activation(out=tvz[G % 4], in_=pqb[:, 0:128], func=AF.Identity, scale=c_zeta[:, h:h + 1]),
                reads=[BPS[1], Bconst], writes=[Btvz[G % 4]])
            ADD("act", lambda e: e.activation(out=tsg[G % 8], in_=pqb[:, 128:256], func=AF.Silu),
                reads=[BPS[1]], writes=[Btsg[G % 8]])

        def S3(G):
            for j in range(2):
                ADD("pe", lambda e, j=j: e.transpose(out=qkT_ps[G % 2][:, j * P:(j + 1) * P], in_=tQK[G % 4][:, j * P:(j + 1) * P],
                                                     identity=identb[:, :]),
                    reads=[BtQK[G % 4], Bcols], writes=[Bqkps[G % 2]])
            ADD("act", lambda e: e.activation(out=tqkT[G % 4], in_=qkT_ps[G % 2], func=AF.Identity),
                reads=[Bqkps[G % 2]], writes=[BtqkT[G % 4]])

        def S5(G):
            h, t = divmod(G, NT)
            qk = tqkT[G % 4]
            ADD("pe", lambda e: e.matmul(sc_ps[G % 2], lhsT=qk[:, 128:256], rhs=qk[:, 0:128], start=True, stop=True),
                reads=[BtqkT[G % 4]], writes=[Bscps[G % 2]])
            ADD("pe", lambda e: e.matmul(kv_ps[G % 2], lhsT=tQK[G % 4][:, 128:256], rhs=tvz[G % 4], start=True, stop=True),
                reads=[BtQK[G % 4], Btvz[G % 4]], writes=[Bkvps[G % 2]])
            ADD("dve", lambda e: e.tensor_tensor(out=tsm[G % 4], in0=sc_ps[G % 2], in1=maskT[:, h, :], op=ALU.mult),
                reads=[Bscps[G % 2], Bconst], writes=[Btsm[G % 4]])
            if t == 0:
                ADD("dve", lambda e: e.tensor_copy(out=Rst[:, :], in_=kv_ps[G % 2]), reads=[Bkvps[G % 2]], writes=[BRst])
            elif t < NT - 1:
                ADD("dve", lambda e: e.scalar_tensor_tensor(out=Rst[:, :], in0=Rst[:, :], scalar=decay[h], in1=kv_ps[G % 2],
                                                            op0=ALU.mult, op1=ALU.add),
                    reads=[Bkvps[G % 2], BRst], writes=[BRst])
            if t < NT - 1:
                ADD("pool", lambda e: e.tensor_copy(out=Rb[:, (G + 1) % 4, :], in_=Rst[:, :]), reads=[BRst], writes=[BRb[(G + 1) % 4]])

        def S7(G):
            h, t = divmod(G, NT)
            ADD("pe", lambda e: e.matmul(out_ps[G % 2], lhsT=tsm[G % 4], rhs=tvb[G % 4], start=True, stop=(t == 0)),
                reads=[Btsm[G % 4], Btvb[G % 4]], writes=[Boutps[G % 2]])
            if t > 0:
                ADD("pe", lambda e: e.matmul(out_ps[G % 2], lhsT=tqkT[G % 4][:, 0:128], rhs=Rb[:, G % 4, :], start=False, stop=True),
                    reads=[BtqkT[G % 4], BRb[G % 4]], writes=[Boutps[G % 2]])
            sl = G % 8
            st = lnst[:, sl, 0:6]
            mv = lnst[:, sl, 6:8]
            lc = c_ln[:, sl * 4:(sl + 1) * 4]
            ADD("dve", lambda e: e.bn_stats(out=st, in_=out_ps[G % 2]), reads=[Boutps[G % 2]], writes=[Bln[sl]])
            ADD("dve", lambda e: e.bn_aggr(out=mv, in_=st), reads=[Bln[sl]], writes=[Bln[sl]])
            ADD("pool", lambda e: e.tensor_tensor(out=lc[:, 0:1], in0=mv[:, 1:2], in1=c_epsp[:, h:h + 1], op=ALU.add),
                reads=[Bln[sl], Bconst], writes=[Bln[sl]])
            ADD("pool", lambda e: e.tensor_tensor(out=lc[:, 1:2], in0=lc[:, 0:1], in1=c_mhalf, op=ALU.pow),
                reads=[Bln[sl], Bcols], writes=[Bln[sl]])

        def S8a(G):
            sl = G % 8
            mv = lnst[:, sl, 6:8]
            lc = c_ln[:, sl * 4:(sl + 1) * 4]
            ADD("dve", lambda e: e.tensor_scalar(out=ty[G % 4], in0=out_ps[G % 2], scalar1=mv[:, 0:1], scalar2=lc[:, 1:2],
                                                 op0=ALU.subtract, op1=ALU.mult),
                reads=[Boutps[G % 2], Bln[sl]], writes=[Bty[G % 4]])

        def S8b(G):
            ADD("pool", lambda e: e.tensor_tensor(out=tyg[G % 4], in0=ty[G % 4], in1=tsg[G % 8], op=ALU.mult),
                reads=[Bty[G % 4], Btsg[G % 8]], writes=[Btyg[G % 4]])

        def S9(G):
            h, t = divmod(G, NT)
            ADD("pe", lambda e: e.transpose(out=ygps[G % 4], in_=tyg[G % 4], identity=identb[:, :]),
                reads=[Btyg[G % 4], Bcols], writes=[Bygps[G % 4]])
            ADD("act", lambda e: e.activation(out=ygT[:, h, t * P:(t + 1) * P], in_=ygps[G % 4], func=AF.Identity),
                reads=[Bygps[G % 4]], writes=[Byg[t // 4]])

        def conv_slot(g):
            return 2 + g % 2

        def conv_pe(g, tb, part):
            cs_ = conv_slot(g)
            W = ring_flat(cs_, 3072).rearrange("p (k n) -> p k n", k=KC)
            seq = [(j, bank, kc) for j, bank in ((1, 6), (2, 7), (0, 5)) for kc in range(KC)]
            for j, bank, kc in seq[part * 6:(part + 1) * 6]:
                ADD("pe", lambda e, j=j, bank=bank, kc=kc: e.matmul(PS[bank][:, :], lhsT=W[:, kc, j * P:(j + 1) * P],
                                                                    rhs=hT[:, kc, tb * 512:(tb + 1) * 512],
                                                                    start=(kc == 0), stop=(kc == KC - 1)),
                    reads=[Bring[cs_]] + BhT[tb * 4:(tb + 1) * 4], writes=[BPS[bank]])

        def conv_ew_chunks(g, tb):
            cur, prv = tb % 2, (tb + 1) % 2
            w0 = c_convw[:, g * 3 + 0:g * 3 + 1]
            w1 = c_convw[:, g * 3 + 1:g * 3 + 2]
            w2 = c_convw[:, g * 3 + 2:g * 3 + 3]

            def chA():
                ADD("act", lambda e: e.activation(out=cv_c, in_=PS[6][:, :], func=AF.Identity), reads=[BPS[6]], writes=[Bcvc])
                if tb == 0:
                    ADD("pool", lambda e: e.memset(cv_u[cur][:, 0:2], 0.0), writes=[Bcvu[cur]])
                else:
                    ADD("pool", lambda e: e.tensor_copy(out=cv_u[cur][:, 0:2], in_=cv_u[prv][:, 512:514]),
                        reads=[Bcvu[prv]], writes=[Bcvu[cur]])
                ADD("dve", lambda e: e.tensor_tensor(out=cv_u[cur][:, 2:514], in0=PS[7][:, :], in1=cv_c, op=ALU.mult),
                    reads=[BPS[7], Bcvc], writes=[Bcvu[cur]])

            def chB():
                ADD("act", lambda e: e.activation(out=cv_b, in_=PS[5][:, :], func=AF.Identity), reads=[BPS[5]], writes=[Bcvb])
                ADD("act", lambda e: e.activation(out=cv_t, in_=cv_u[cur][:, 2:514], func=AF.Identity, scale=w2),
                    reads=[Bcvu[cur], Bconst], writes=[Bcvt])

            def chC():
                ADD("dve", lambda e: e.scalar_tensor_tensor(out=cv_t, in0=cv_u[cur][:, 1:513], scalar=w1, in1=cv_t,
                                                            op0=ALU.mult, op1=ALU.add),
                    reads=[Bcvu[cur], Bcvt, Bconst], writes=[Bcvt])
                ADD("dve", lambda e: e.scalar_tensor_tensor(out=cv_t, in0=cv_u[cur][:, 0:512], scalar=w0, in1=cv_t,
                                                            op0=ALU.mult, op1=ALU.add),
                    reads=[Bcvu[cur], Bcvt, Bconst], writes=[Bcvt])

            def chD():
                ADD("pool", lambda e: e.tensor_tensor(out=bcT[:, g, tb * 512:(tb + 1) * 512], in0=cv_b, in1=cv_t, op=ALU.mult),
                    reads=[Bcvb, Bcvt], writes=[Bbc[tb]])
            return [chA, chB, chC, chD]

        P2SLOT = [2, 3, 0, 1, 2, 3, 0, 1]

        def load_p2(dc):
            s_ = P2SLOT[dc]
            wload(ring_flat(s_), wp2_d[dc * P:(dc + 1) * P, :], [Bring[s_]], "ring%d" % s_)

        adax = Ub[:, 8192:12288]
        Badax = Buf("adax")

        def ada_f_block(i_):
            blk = 6 + i_
            wload(adax, adaw_d[blk * P:(blk + 1) * P, :], [Badax], "adax")

        def ada_f_mms(i_):
            W_ = adax.rearrange("p (k n) -> p k n", k=KC)
            for jj in range(4):
                for kc in range(KC):
                    ADD("pe", lambda e, jj=jj, kc=kc: e.matmul(PS[4][:, 256 + i_ * 4 + jj:256 + i_ * 4 + jj + 1],
                                                               lhsT=W_[:, kc, jj * P:(jj + 1) * P],
                                                               rhs=csbf[:, kc:kc + 1], start=(kc == 0), stop=(kc == KC - 1)),
                        reads=[Badax, Bcsbf], writes=[BPS[4]])

        NG1 = H * NT
        pending = {}
        for j in range(NG1 + 8):
            if j < NG1:
                h, t = divmod(j, NT)
                S1(j)
                S2(j)
            if 0 <= j - 7 < NG1:
                S9(j - 7)
            if j < NG1:
                conv_pe(h, t // 4, t % 4)
                if t % 4 == 3:
                    for ci_, ch in enumerate(conv_ew_chunks(h, t // 4)):
                        pending.setdefault(j + ci_, []).append(ch)
                if t == 4 and h + 1 < H:
                    if h >= 1:
                        load_qkvg(h + 1)
                    load_conv(h + 1)
                if h < 4 and t == 2:
                    ada_f_block(h)
                if h < 4 and t == 10:
                    ada_f_mms(h)
                if h == 7 and t == 8:
                    load_p2(0)
                if h == 7 and t == 12:
                    load_p2(2)
            if 0 <= j - 1 < NG1:
                S3(j - 1)
            if 0 <= j - 2 < NG1:
                S5(j - 2)
            if 0 <= j - 3 < NG1:
                S7(j - 3)
            if 0 <= j - 4 < NG1:
                S8a(j - 4)
            if 0 <= j - 5 < NG1:
                S8b(j - 5)
            for ch in pending.pop(j, []):
                ch()
        assert not pending
        ADD("dve", lambda e: e.tensor_tensor(out=c_B2, in0=PS[4][:, 256:264], in1=c_adab[:, 24:32], op=ALU.add),
            reads=[BPS[4], Bconst], writes=[BAB2])
        ADD("dve", lambda e: e.tensor_tensor(out=tmp8[:, :], in0=PS[4][:, 264:272], in1=c_adab[:, 32:40], op=ALU.add),
            reads=[BPS[4], Bconst, BAB1], writes=[BAB2])
        ADD("dve", lambda e: e.scalar_tensor_tensor(out=c_A2, in0=tmp8[:, :], scalar=1.0, in1=c_gffn,
                                                    op0=ALU.add, op1=ALU.mult), reads=[BAB2, Bconst], writes=[BAB2])
        if stage == 1:
            if dbg:
                dbg_ops.append(ADD("sp", lambda e: e.dma_start(out=dbg_d[:, 0:32768], in_=BCb[:, :]), reads=Byg + Bbc, dma_key="dbg"))
            finish()
            return nc

        handoff(e_p1, Bmg)
        p2t = [[U[:, 5120 + (2 * s + i) * 512: 5120 + (2 * s + i + 1) * 512] for i in range(2)] for s in range(2)]
        Bp2t = [[Buf("p2t%d%d" % (s, i)) for i in range(2)] for s in range(2)]
        handoff([Brope], [Bgtm, Bgtf, Bgfin])
        ADD("sp", lambda e: e.dma_start(out=gtm, in_=adabr_d[0:1, 2 * D:3 * D].broadcast_to([P, D])), writes=[Bgtm], dma_key="gtm")
        ADD("sp", lambda e: e.dma_start(out=gtf, in_=adabr_d[0:1, 5 * D:6 * D].broadcast_to([P, D])), writes=[Bgtf], dma_key="gtf")
        ADD("sp", lambda e: e.dma_start(out=gfin, in_=gfin_d[0:1, :].broadcast_to([P, D])), writes=[Bgfin], dma_key="gfin")
        cnt = 0
        load_p2(1)
        for dc in range(KC):
            if dc >= 1 and dc + 2 < KC:
                load_p2(dc + 2)
            if dc == 5:
                load_ada(4, 2)
            if dc == 6:
                load_ada(5, 3)
            if dc == 7:
                wload(ring_flat(0), mixl_d[0:P, :], [Bring[0]], "ring0")
            s = P2SLOT[dc]
            W = ring_slot(s)
            for tb in range(4):
                st = cnt % 2
                cnt += 1
                bk = [0, 1, 2, 3] if st == 0 else [4, 5, 6, 7]
                tsl = slice(tb * 512, (tb + 1) * 512)
                for kc in range(KC):
                    ADD("pe", lambda e, kc=kc, b=bk[0], tsl=tsl, W=W: e.matmul(PS[b][:, :], lhsT=W[:, kc, 0:128], rhs=ygT[:, kc, tsl],
                                                                          start=(kc == 0), stop=(kc == KC - 1)),
                        reads=[Bring[s], Byg[tb]], writes=[BPS[bk[0]]])
                for kc in range(KC):
                    ADD("pe", lambda e, kc=kc, b=bk[1], tsl=tsl, W=W: e.matmul(PS[b][:, :], lhsT=W[:, kc, 128:256], rhs=bcT[:, kc, tsl],
                                                                          start=(kc == 0), stop=(kc == KC - 1)),
                        reads=[Bring[s], Bbc[tb]], writes=[BPS[bk[1]]])
                for kc in range(KC):
                    ADD("pe", lambda e, kc=kc, b=bk[2], tsl=tsl, W=W: e.matmul(PS[b][:, :], lhsT=W[:, kc, 256:384], rhs=hT[:, kc, tsl],
                                                                          start=(kc == 0), stop=(kc == KC - 1)),
                        reads=[Bring[s]] + BhT[tb * 4:(tb + 1) * 4], writes=[BPS[bk[2]]])
                for kc in range(KC):
                    ADD("pe", lambda e, kc=kc, b=bk[3], tsl=tsl, W=W: e.matmul(PS[b][:, :], lhsT=W[:, kc, 384:512], rhs=hT[:, kc, tsl],
                                                                          start=(kc == 0), stop=(kc == KC - 1)),
                        reads=[Bring[s]] + BhT[tb * 4:(tb + 1) * 4], writes=[BPS[bk[3]]])
                sa, sbb = p2t[st]
                ADD("act", lambda e, sa=sa, b=bk[2]: e.activation(out=sa, in_=PS[b][:, :], func=AF.Sigmoid),
                    reads=[BPS[bk[2]]], writes=[Bp2t[st][0]])
                ADD("act", lambda e, sbb=sbb, b=bk[3]: e.activation(out=sbb, in_=PS[b][:, :], func=AF.Sigmoid),
                    reads=[BPS[bk[3]]], writes=[Bp2t[st][1]])
                ADD("dve", lambda e, sa=sa, b=bk[0]: e.tensor_tensor(out=sa, in0=PS[b][:, :], in1=sa, op=ALU.mult),
                    reads=[BPS[bk[0]], Bp2t[st][0]], writes=[Bp2t[st][0]])
                ADD("dve", lambda e, sbb=sbb, b=bk[1]: e.tensor_tensor(out=sbb, in0=PS[b][:, :], in1=sbb, op=ALU.mult),
                    reads=[BPS[bk[1]], Bp2t[st][1]], writes=[Bp2t[st][1]])
                ADD("pool", lambda e, sa=sa, sbb=sbb, dc=dc, tsl=tsl: e.tensor_tensor(out=mergedT[:, dc, tsl], in0=sa, in1=sbb, op=ALU.add),
                    reads=[Bp2t[st][0], Bp2t[st][1]], writes=[Bmg[tb]])

        if stage == 2:
            if dbg:
                dbg_ops.append(ADD("sp", lambda e: e.dma_start(out=dbg_d[:, 0:16384], in_=Eb[:, :]), reads=Bmg, dma_key="dbg"))
            finish()
            return nc

        wload(ring_flat(1), mixl_d[P:2 * P, :], [Bring[1]], "ring1")
        ada_bcast(0, gtm, Bgtm, 2, 0)
        ada_bcast(1, gtm, Bgtm, 3, 1)
        load_ada(10, 2)
        load_ada(11, 3)

        def load_gu(g):
            start, n = FFN_GROUPS[g]
            rb = (g + 1) % 2
            bufs = [Bring[2 * rb], Bring[2 * rb + 1]]
            wload(ring[:, rb * 8192:rb * 8192 + 2048 * n], wgu_d[g][:, :], bufs, "ring%d" % (2 * rb))

        handoff(Byg + Bbc, Bx1)
        Bxs2 = [Buf("xs2_0"), Buf("xs2_1")]
        handoff([Bconst, Badax], Bxs2)
        xs2 = [U[:, 3072:4096], U[:, 4096:5120]]
        xn2 = [U[:, 5120 + i * 1024:5120 + (i + 1) * 1024] for i in range(4)]
        Bxn2 = [Buf("xn2_%d" % i) for i in range(4)]
        handoff(Bp2t[0] + Bp2t[1], Bxn2)

        def p4_A(t):
            norm_A(t, x1[:, t, :], [Bx1[t]], xn2[t % 4], Bxn2[t % 4])

        def p4_B(tb):
            norm_flush()
            ts_ = range(tb * 4, tb * 4 + 4)
            norm_B_block(tb, [xn2[t % 4] for t in ts_], [Bxn2[t % 4] for t in ts_], c_A2, c_B2, BAB2)

        cnt = 0
        for t in range(NT):
            s2 = t % 2
            ADD("sp", lambda e, t=t, s2=s2: e.dma_start(out=xs2[s2], in_=x_d[t * P:(t + 1) * P, :]), writes=[Bxs2[s2]], dma_key="xs2_%d" % s2)
            for hf in range(2):
                bank = 2 + cnt % 6
                cnt += 1
                hs = slice(hf * 512, (hf + 1) * 512)
                Wm = ring_slot(hf)
                for dc in range(KC):
                    ADD("pe", lambda e, dc=dc, bank=bank, Wm=Wm, t=t: e.matmul(PS[bank][:, :], lhsT=mergedT[:, dc, t * P:(t + 1) * P],
                                                                              rhs=Wm[:, dc, :], start=(dc == 0), stop=(dc == KC - 1)),
                        reads=[Bmg[t // 4], Bring[hf]], writes=[BPS[bank]])
                ADD("dve", lambda e, bank=bank, t=t, hs=hs: e.tensor_tensor(out=x1[:, t, hs], in0=PS[bank][:, :], in1=gtm[:, hs], op=ALU.mult),
                    reads=[BPS[bank], Bgtm], writes=[Bx1[t]])
                ADD("pool", lambda e, t=t, hs=hs, s2=s2: e.tensor_tensor(out=x1[:, t, hs], in0=x1[:, t, hs], in1=xs2[s2][:, hs], op=ALU.add),
                    reads=[Bx1[t], Bxs2[s2]], writes=[Bx1[t]])
            if t >= 5 and (t - 5) % 4 == 0:
                p4_B((t - 5) // 4)
            if t >= 1:
                p4_A(t - 1)
            if t == 3:
                ada_bcast(0, gtf, Bgtf, 2, 0)
                ada_bcast(1, gtf, Bgtf, 3, 1)
                load_gu(0)
        p4_A(NT - 1)
        if stage == 3:
            for t in range(NT):
                final_waits.append(ADD("sp", lambda e, t=t: e.dma_start(out=out_d[t * P:(t + 1) * P, :], in_=x1[:, t, :]),
                                       reads=[Bx1[t]], dma_key="out"))
            finish()
            return nc

        wdv = wd_d.rearrange("(c p) n -> p c n", p=P)
        Wdb = [Ub[:, 10240:14336].rearrange("p (c n) -> p c n", c=4), Ub[:, 14336:18432].rearrange("p (c n) -> p c n", c=4)]
        BWdb = [Buf("wdb0"), Buf("wdb1")]
        stg = [U[:, 3072:4096], U[:, 4096:5120]]
        Bstg = [Buf("stg0"), Buf("stg1")]
        sgf = [U[:, 0:512], U[:, 512:1024]]
        Bsgf = [Buf("sgf0"), Buf("sgf1")]
        actT = [Eb[:, 0:8192].rearrange("p (c s) -> p c s", c=4), Eb[:, 8192:16384].rearrange("p (c s) -> p c s", c=4)]
        Bact = [[Buf("act%d_%d" % (i, tb)) for tb in range(4)] for i in range(2)]

        stg_cnt = [0]

        def load_wd(g):
            start, n = FFN_GROUPS[g]
            for ci in range(n):
                k = stg_cnt[0] % 2
                stg_cnt[0] += 1
                ADD("sp", lambda e, k=k, c=start + ci: e.dma_start(out=stg[k], in_=wdv[:, c, :]), writes=[Bstg[k]], dma_key="stg%d" % k)
                ADD("pool", lambda e, k=k, ci=ci, g=g: e.tensor_tensor(out=Wdb[g % 2][:, ci, :], in0=stg[k], in1=gtf, op=ALU.mult),
                    reads=[Bstg[k], Bgtf], writes=[BWdb[g % 2]])

        handoff(Bxs2, Bstg)
        handoff([Bgtm], Bsgf)
        handoff(Bmg, Bact[0] + Bact[1])
        load_gu(1)

        if stage == 4:
            if dbg:
                dbg_ops.append(ADD("sp", lambda e: e.dma_start(out=dbg_d[:, 0:16384], in_=hT[:, :, :].rearrange("p k s -> p (k s)")),
                                   reads=BhT, dma_key="dbg"))
            finish()
            return nc

        gu_cnt = [0]
        dn_cnt = [0]

        def gu(g, tb):
            start, n = FFN_GROUPS[g]
            b = g % 2
            rb = (g + 1) % 2
            W = ring[:, rb * 8192:rb * 8192 + 2048 * n].rearrange("p (k n) -> p k n", k=KC)
            tsl = slice(tb * 512, (tb + 1) * 512)
            for ci in range(n):
                st = gu_cnt[0] % 2
                gu_cnt[0] += 1
                gb, ub = (0, 1) if st == 0 else (2, 3)
                for kc in range(KC):
                    ADD("pe", lambda e, kc=kc, ci=ci, gb=gb: e.matmul(PS[gb][:, :], lhsT=W[:, kc, ci * P:(ci + 1) * P], rhs=hT[:, kc, tsl],
                                                                      start=(kc == 0), stop=(kc == KC - 1)),
                        reads=[Bring[2 * rb], Bring[2 * rb + 1]] + BhT[tb * 4:(tb + 1) * 4], writes=[BPS[gb]])
                for kc in range(KC):
                    ADD("pe", lambda e, kc=kc, ci=ci, ub=ub, n=n: e.matmul(PS[ub][:, :], lhsT=W[:, kc, (n + ci) * P:(n + ci + 1) * P], rhs=hT[:, kc, tsl],
                                                                      start=(kc == 0), stop=(kc == KC - 1)),
                        reads=[Bring[2 * rb], Bring[2 * rb + 1]] + BhT[tb * 4:(tb + 1) * 4], writes=[BPS[ub]])
                ADD("act", lambda e, st=st, gb=gb: e.activation(out=sgf[st], in_=PS[gb][:, :], func=AF.Silu),
                    reads=[BPS[gb]], writes=[Bsgf[st]])
                ADD("dve", lambda e, st=st, ub=ub, ci=ci: e.tensor_tensor(out=actT[b][:, ci, tsl], in0=PS[ub][:, :], in1=sgf[st], op=ALU.mult),
                    reads=[BPS[ub], Bsgf[st]], writes=[Bact[b][tb]])

        def down(g, tb):
            start, n = FFN_GROUPS[g]
            b = g % 2
            for t in range(tb * 4, tb * 4 + 4):
                for hf in range(2):
                    bank = 4 + dn_cnt[0] % 4
                    dn_cnt[0] += 1
                    hs = slice(hf * 512, (hf + 1) * 512)
                    for ci in range(n):
                        ADD("pe", lambda e, ci=ci, bank=bank, t=t, hs=hs: e.matmul(PS[bank][:, :], lhsT=actT[b][:, ci, t * P:(t + 1) * P],
                                                                                  rhs=Wdb[b][:, ci, hs], start=(ci == 0), stop=(ci == n - 1)),
                            reads=[Bact[b][tb], BWdb[b]], writes=[BPS[bank]])
                    ADD("dve", lambda e, bank=bank, t=t, hs=hs: e.tensor_tensor(out=x1[:, t, hs], in0=PS[bank][:, :], in1=x1[:, t, hs], op=ALU.add),
                        reads=[BPS[bank], Bx1[t]], writes=[Bx1[t]])

        def final_block(tb):
            ts_ = list(range(tb * 4, tb * 4 + 4))
            for t in ts_:
                ADD("act", lambda e, t=t: e.activation(out=stg[t % 2], in_=x1[:, t, :], func=AF.Square, accum_out=c_ss[:, t:t + 1]),
                    reads=[Bx1[t], Bstg[t % 2]], writes=[Bstg[t % 2], Bss[t]])
            for t in ts_:
                ADD("dve", lambda e, t=t: e.tensor_scalar(out=c_tmp[:, t:t + 1], in0=c_ss[:, t:t + 1], scalar1=float(1.0 / D),
                                                          scalar2=float(EPS), op0=ALU.mult, op1=ALU.add), reads=[Bss[t]], writes=[Bss[t]])
            for t in ts_:
                ADD("pool", lambda e, t=t: e.tensor_tensor(out=c_rs[:, t:t + 1], in0=c_tmp[:, t:t + 1], in1=c_mhalf, op=ALU.pow),
                    reads=[Bss[t], Bcols], writes=[Bss[t]])
            for t in ts_:
                ADD("dve", lambda e, t=t: e.scalar_tensor_tensor(out=x1[:, t, :], in0=x1[:, t, :], scalar=c_rs[:, t:t + 1], in1=gfin,
                                                                 op0=ALU.mult, op1=ALU.mult), reads=[Bx1[t], Bss[t], Bgfin], writes=[Bx1[t]])
                final_waits.append(ADD("sp", lambda e, t=t: e.dma_start(out=out_d[t * P:(t + 1) * P, :], in_=x1[:, t, :]),
                                       reads=[Bx1[t]], dma_key="out"))

        def down_merged(gs, tb):
            chunks = [(g_ % 2, ci) for g_ in gs for ci in range(FFN_GROUPS[g_][1])]
            for t in range(tb * 4, tb * 4 + 4):
                for hf in range(2):
                    bank = 4 + dn_cnt[0] % 4
                    dn_cnt[0] += 1
                    hs = slice(hf * 512, (hf + 1) * 512)
                    for k_, (b_, ci) in enumerate(chunks):
                        ADD("pe", lambda e, ci=ci, b_=b_, bank=bank, t=t, hs=hs, k_=k_: e.matmul(
                            PS[bank][:, :], lhsT=actT[b_][:, ci, t * P:(t + 1) * P], rhs=Wdb[b_][:, ci, hs],
                            start=(k_ == 0), stop=(k_ == len(chunks) - 1)),
                            reads=[Bact[0][tb], Bact[1][tb], BWdb[0], BWdb[1]], writes=[BPS[bank]])
                    ADD("dve", lambda e, bank=bank, t=t, hs=hs: e.tensor_tensor(out=x1[:, t, hs], in0=PS[bank][:, :], in1=x1[:, t, hs], op=ALU.add),
                        reads=[BPS[bank], Bx1[t]], writes=[Bx1[t]])

        NG = len(FFN_GROUPS)
        for g in range(NG):
            if g >= 1:
                if g + 1 < NG:
                    load_gu(g + 1)
            for tb in range(4):
                gu(g, tb)
                if g == 0 and tb == 0:
                    p4_B(3)
                    handoff(Bp2t[0] + Bp2t[1] + Bxn2, BWdb)
                    load_wd(0)
                if 1 <= g <= NG - 2:
                    down(g - 1, tb)
            if g + 1 < NG:
                load_wd(g + 1)
        for tb in range(4):
            down_merged([NG - 2, NG - 1], tb)
            final_block(tb)
        finish()
    return nc


def _consts():
    h = np.arange(H, dtype=np.float64)
    log_gamma = np.log(1.0 - 2.0 ** (-5.0 - h))
    idx = np.arange(P, dtype=np.float64)
    dscale = float(P) ** -0.5
    mask = np.zeros((P, H, P), dtype=np.float64)
    for hh in range(H):
        col = dscale * np.exp(-log_gamma[hh] * (idx + 1.0))
        mm = np.where(idx[None, :] >= idx[:, None], 1.0, 0.0)
        mask[:, hh, :] = mm * col[:, None]
    zeta = dscale * np.exp(log_gamma[None, :] * (P - 1 - idx)[:, None])
    xi = np.exp(log_gamma[None, :] * (idx + 1.0)[:, None])
    epsp = EPS / (xi * xi)
    inv_freq = 1.0 / (10000.0 ** (np.arange(0, P, 2, dtype=np.float32) / np.float32(P)))
    invf = np.broadcast_to(inv_freq.astype(np.float32)[None, :], (P, 64))
    return (np.ascontiguousarray(mask.reshape(P, H * P).astype(np.float32)), np.ascontiguousarray(zeta.astype(np.float32)),
            np.ascontiguousarray(epsp.astype(np.float32)), np.ascontiguousarray(invf.astype(np.float32)),
            np.eye(P, dtype=np.float32))


def make_in_maps(x, c, positions, ada_w, ada_b, norm_mix_g, w_in, conv_w, ret_w_out, conv_w_out, mix_w_out,
                 norm_ffn_g, ffn_w_gate, ffn_w_up, ffn_w_down, final_norm_g, n_cores=8):
    f = lambda a: np.ascontiguousarray(np.asarray(a, dtype=np.float32))
    mask, zeta, epsp, invf, ident = _consts()
    col8 = lambda v: np.ascontiguousarray(np.asarray(v, dtype=np.float32).reshape(-1, P).T)
    adaw = f(ada_w[0]); win = f(w_in[0])
    adaw_l = adaw.reshape(8, P, 12, 512).transpose(2, 1, 0, 3).reshape(12 * P, 4096)
    w1_l = win[:, 0:4096].reshape(8, P, 4, 8, P).transpose(3, 1, 0, 2, 4).reshape(8 * P, 4096)
    wc_l = win[:, 4096:7168].reshape(8, P, 3, 8, P).transpose(3, 1, 0, 2, 4).reshape(8 * P, 3072)
    st4 = np.stack([f(ret_w_out[0]), f(conv_w_out[0]), win[:, 7168:8192], win[:, 8192:9216]], axis=1)
    wp2_l = st4.reshape(8, P, 4, 8, P).transpose(3, 1, 0, 2, 4).reshape(8 * P, 4096)
    mix_l = f(mix_w_out[0]).reshape(8, P, 2, 512).transpose(2, 1, 0, 3).reshape(2 * P, 4096)
    wg = f(ffn_w_gate[0]).reshape(8, P, HID); wu = f(ffn_w_up[0]).reshape(8, P, HID)
    shared = {
        "adaw_l": np.ascontiguousarray(adaw_l), "adab_col": col8(ada_b[0]), "adab_row": f(ada_b[0]).reshape(1, -1),
        "gmix_col": col8(norm_mix_g[0]), "gffn_col": col8(norm_ffn_g[0]), "gfin_row": f(final_norm_g).reshape(1, -1),
        "w1_l": np.ascontiguousarray(w1_l), "wc_l": np.ascontiguousarray(wc_l), "wp2_l": np.ascontiguousarray(wp2_l),
        "mix_l": np.ascontiguousarray(mix_l),
        "convw_col": np.ascontiguousarray(f(conv_w[0]).reshape(3, 8, P).transpose(2, 1, 0).reshape(P, 24)),
        "ffn_w_down": f(ffn_w_down[0]),
        "invf": invf, "maskT": mask, "zeta": zeta, "epsp": epsp, "ident_in": ident,
    }
    for g, (st_, n) in enumerate(FFN_GROUPS):
        gg = wg[:, :, st_ * P:(st_ + n) * P]; uu = wu[:, :, st_ * P:(st_ + n) * P]
        shared["wgu_l%d" % g] = np.ascontiguousarray(np.stack([gg, uu], axis=2).transpose(1, 0, 2, 3).reshape(P, 2048 * n))
    maps = []
    xs = np.asarray(x, dtype=np.float32)
    cs = np.asarray(c, dtype=np.float32)
    ps = np.asarray(positions, dtype=np.int32)
    for b in range(n_cores):
        m = dict(shared)
        m["x"] = np.ascontiguousarray(xs[b])
        m["ccol"] = col8(cs[b])
        m["pos"] = np.ascontiguousarray(ps[b].reshape(NT, P).T)
        maps.append(m)
    return maps


_NC_CACHE = {}


def kernel(**inputs):
    if "nc" not in _NC_CACHE:
        _NC_CACHE["nc"] = build_nc()
    nc = _NC_CACHE["nc"]
    in_maps = make_in_maps(**inputs)
    res = run_bass_kernel_spmd(nc, in_maps, core_ids=list(range(8)))
    out = np.stack([np.asarray(r["out"], dtype=np.float32) for r in res.results], axis=0)
    return out
```
